# Optimizing a Trainium2 kernel written in Bass

```python
import math
import jax, jax.numpy as jnp
from jax import lax
import numpy as np

D_MODEL = 1024
BATCH = 8
SEQ = 4096
DEPTH = 4

EPS = 1e-6
GLA_HEADS = 4
GLA_DK = D_MODEL // 2 // GLA_HEADS
GLA_DV = D_MODEL // GLA_HEADS
GLA_KEY = GLA_HEADS * GLA_DK
GLA_VAL = GLA_HEADS * GLA_DV
GLA_GATE_RANK = 16
GLA_GATE_NORM = 16.0
GLA_CHUNK = 64
SSM_EXPAND = 2
SSM_INNER = SSM_EXPAND * D_MODEL
SSM_HEADDIM = 64
SSM_HEADS = SSM_INNER // SSM_HEADDIM
SSM_GROUPS = 4
SSM_STATE = 128
SSM_CONV = 4
SSM_CHUNK = 128
SSM_XBC = SSM_INNER + 2 * SSM_GROUPS * SSM_STATE
SSM_DT_MIN = 0.001
SSM_DT_MAX = 0.1
FFN_HIDDEN = ((8 * D_MODEL // 3 + 255) // 256) * 256
IN_SPLITS = (GLA_KEY, GLA_KEY, GLA_VAL, GLA_VAL, GLA_GATE_RANK, SSM_INNER, SSM_XBC, SSM_HEADS, D_MODEL, D_MODEL)
IN_DIM = sum(IN_SPLITS)

kernel_name = "hybrid_gla_ssd_gated_merge"


def split_cols(u, sizes):
    idx, acc = [], 0
    for s in sizes[:-1]:
        acc += s
        idx.append(acc)
    return jnp.split(u, idx, axis=-1)


def rms_norm(x, w):
    xf = x.astype(jnp.float32)
    y = xf * lax.rsqrt(jnp.mean(xf * xf, axis=-1, keepdims=True) + EPS)
    return (y * w.astype(jnp.float32)).astype(x.dtype)


def gla_chunked(q, k, v, g):
    Bsz, T, H, dk = q.shape
    dv = v.shape[-1]
    n = T // GLA_CHUNK

    def to_chunks(a):
        return a.reshape(Bsz, n, GLA_CHUNK, H, a.shape[-1]).transpose(1, 0, 3, 2, 4)

    qc, kc, vc, gc = to_chunks(q * (GLA_DK ** -0.5)), to_chunks(k), to_chunks(v), to_chunks(g)
    causal = jnp.tril(jnp.ones((GLA_CHUNK, GLA_CHUNK), dtype=bool))

    def step(S, inp):
        qi, ki, vi, gi = inp
        b = jnp.cumsum(gi, axis=2)
        b_last = b[:, :, -1:, :]
        o_inter = jnp.einsum('bhcd,bhde->bhce', qi * jnp.exp(b), S)
        diff = b[:, :, :, None, :] - b[:, :, None, :, :]
        decay = jnp.exp(jnp.where(causal[:, :, None], diff, -jnp.inf))
        A = jnp.einsum('bhid,bhjd,bhijd->bhij', qi, ki, decay)
        o = o_inter + jnp.einsum('bhij,bhje->bhie', A, vi)
        S = jnp.exp(b_last)[:, :, 0, :, None] * S + jnp.einsum('bhcd,bhce->bhde', ki * jnp.exp(b_last - b), vi)
        return S, o

    S0 = jnp.zeros((Bsz, H, dk, dv), jnp.float32)
    _, o = lax.scan(step, S0, (qc, kc, vc, gc))
    return o.transpose(1, 0, 3, 2, 4).reshape(Bsz, T, H, dv)


def ssd_chunked(x, dt, A, Bm, Cm):
    Bsz, T, H, P = x.shape
    G, N = Bm.shape[2], Bm.shape[3]
    E = H // G
    n = T // SSM_CHUNK
    L = SSM_CHUNK
    xc = x.reshape(Bsz, n, L, G, E, P).transpose(1, 0, 2, 3, 4, 5)
    ac = (dt * A).reshape(Bsz, n, L, G, E).transpose(1, 0, 3, 4, 2)
    dtc = dt.reshape(Bsz, n, L, G, E).transpose(1, 0, 2, 3, 4)
    Bc = Bm.reshape(Bsz, n, L, G, N).transpose(1, 0, 2, 3, 4)
    Cc = Cm.reshape(Bsz, n, L, G, N).transpose(1, 0, 2, 3, 4)
    causal = jnp.tril(jnp.ones((L, L), dtype=bool))

    def step(S, inp):
        xi, ai, dti, Bi, Ci = inp
        cum = jnp.cumsum(ai, axis=-1)
        seg = cum[..., :, None] - cum[..., None, :]
        Lmat = jnp.exp(jnp.where(causal, seg, -jnp.inf))
        CB = jnp.einsum('blgn,bsgn->bgls', Ci, Bi)
        xdt = xi * dti[..., None]
        y_diag = jnp.einsum('bgls,bgels,bsgep->blgep', CB, Lmat, xdt)
        y_off = jnp.einsum('blgn,bgepn,bgel->blgep', Ci, S, jnp.exp(cum))
        decay_state = jnp.exp(cum[..., -1:] - cum)
        S = jnp.exp(cum[..., -1])[..., None, None] * S + jnp.einsum('bsgn,bges,bsgep->bgepn', Bi, decay_state, xdt)
        return S, y_diag + y_off

    S0 = jnp.zeros((Bsz, G, E, P, N), jnp.float32)
    _, y = lax.scan(step, S0, (xc, ac, dtc, Bc, Cc))
    return y.transpose(1, 0, 2, 3, 4, 5).reshape(Bsz, T, H, P)


def causal_depthwise_conv(u, w, b):
    C = u.shape[-1]
    y = lax.conv_general_dilated(u, w[:, None, :].astype(u.dtype), window_strides=(1,),
                                 padding=((SSM_CONV - 1, 0),), dimension_numbers=('NWC', 'WIO', 'NWC'),
                                 feature_group_count=C)
    return y + b.astype(u.dtype)


def hybrid_mixer(h, w_in, gla_w2, gla_b, gla_norm_w, conv_w, conv_b, dt_bias, A_log, D_skip,
                 ssm_norm_w, w_ya, w_yb, w_out):
    Bsz, T, _ = h.shape
    f32 = lambda t: t.astype(jnp.float32)
    u = h @ w_in
    q, k, v, r, glr, z, xbc, dt_raw, ga, gb = split_cols(u, IN_SPLITS)

    g = jax.nn.log_sigmoid(f32(glr @ gla_w2 + gla_b)) / GLA_GATE_NORM
    hk = lambda t, d: f32(t).reshape(Bsz, T, GLA_HEADS, d)
    o = gla_chunked(hk(q, GLA_DK), hk(k, GLA_DK), hk(v, GLA_DV), g.reshape(Bsz, T, GLA_HEADS, GLA_DK))
    o = rms_norm(o, gla_norm_w).reshape(Bsz, T, GLA_VAL)
    y_a = (o * jax.nn.silu(f32(r))).astype(h.dtype) @ w_ya

    xbc = jax.nn.silu(causal_depthwise_conv(xbc, conv_w, conv_b))
    xs, Bm, Cm = split_cols(xbc, (SSM_INNER, SSM_GROUPS * SSM_STATE, SSM_GROUPS * SSM_STATE))
    dt = jax.nn.softplus(f32(dt_raw) + f32(dt_bias))
    A = -jnp.exp(f32(A_log))
    xh = f32(xs).reshape(Bsz, T, SSM_HEADS, SSM_HEADDIM)
    y = ssd_chunked(xh, dt, A, f32(Bm).reshape(Bsz, T, SSM_GROUPS, SSM_STATE),
                    f32(Cm).reshape(Bsz, T, SSM_GROUPS, SSM_STATE))
    y = y + xh * f32(D_skip)[None, None, :, None]
    y = y.reshape(Bsz, T, SSM_INNER) * jax.nn.silu(f32(z))
    gsz = SSM_INNER // SSM_GROUPS
    y = rms_norm(y.reshape(Bsz, T, SSM_GROUPS, gsz), ssm_norm_w.reshape(SSM_GROUPS, gsz)).reshape(Bsz, T, SSM_INNER)
    y_b = y.astype(h.dtype) @ w_yb

    m = jax.nn.sigmoid(ga) * y_a + jax.nn.sigmoid(gb) * y_b
    return m @ w_out


def swiglu(h, w_in, w_out):
    gate, up = split_cols(h @ w_in, (FFN_HIDDEN, FFN_HIDDEN))
    return (jax.nn.silu(gate) * up) @ w_out


def setup_inputs(seed: int = 0) -> dict:
    key = jax.random.key(seed)
    ks = jax.random.split(key, 24)
    nrm = lambda k, shape, s: jax.random.normal(k, shape, jnp.float32) * s
    Lh = (DEPTH,)
    u_dt = jax.random.uniform(ks[9], Lh + (SSM_HEADS,), jnp.float32)
    dt0 = jnp.exp(u_dt * (math.log(SSM_DT_MAX) - math.log(SSM_DT_MIN)) + math.log(SSM_DT_MIN))
    return {
        "x": nrm(ks[0], (BATCH, SEQ, D_MODEL), 1.0),
        "norm1_w": 1.0 + nrm(ks[1], Lh + (D_MODEL,), 0.01),
        "w_in": nrm(ks[2], Lh + (D_MODEL, IN_DIM), D_MODEL ** -0.5),
        "gla_gate_w2": nrm(ks[3], Lh + (GLA_GATE_RANK, GLA_KEY), GLA_GATE_RANK ** -0.5),
        "gla_gate_b": nrm(ks[4], Lh + (GLA_KEY,), 0.1),
        "gla_norm_w": 1.0 + nrm(ks[5], Lh + (GLA_DV,), 0.01),
        "ssm_conv_w": nrm(ks[6], Lh + (SSM_CONV, SSM_XBC), SSM_CONV ** -0.5),
        "ssm_conv_b": nrm(ks[7], Lh + (SSM_XBC,), 0.01),
        "ssm_dt_bias": dt0 + jnp.log(-jnp.expm1(-dt0)),
        "ssm_A_log": jnp.log(jax.random.uniform(ks[10], Lh + (SSM_HEADS,), jnp.float32, 1.0, 16.0)),
        "ssm_D": 1.0 + nrm(ks[11], Lh + (SSM_HEADS,), 0.1),
        "ssm_norm_w": 1.0 + nrm(ks[12], Lh + (SSM_INNER,), 0.01),
        "w_branch_a": nrm(ks[13], Lh + (GLA_VAL, D_MODEL), GLA_VAL ** -0.5),
        "w_branch_b": nrm(ks[14], Lh + (SSM_INNER, D_MODEL), SSM_INNER ** -0.5),
        "w_mix_out": nrm(ks[15], Lh + (D_MODEL, D_MODEL), D_MODEL ** -0.5),
        "norm2_w": 1.0 + nrm(ks[16], Lh + (D_MODEL,), 0.01),
        "w_ffn_in": nrm(ks[17], Lh + (D_MODEL, 2 * FFN_HIDDEN), D_MODEL ** -0.5),
        "w_ffn_out": nrm(ks[18], Lh + (FFN_HIDDEN, D_MODEL), FFN_HIDDEN ** -0.5),
        "final_norm_w": 1.0 + nrm(ks[19], (D_MODEL,), 0.01),
    }


def reference(x, norm1_w, w_in, gla_gate_w2, gla_gate_b, gla_norm_w, ssm_conv_w, ssm_conv_b,
              ssm_dt_bias, ssm_A_log, ssm_D, ssm_norm_w, w_branch_a, w_branch_b, w_mix_out,
              norm2_w, w_ffn_in, w_ffn_out, final_norm_w):
    for l in range(DEPTH):
        h = rms_norm(x, norm1_w[l])
        x = x + hybrid_mixer(h, w_in[l], gla_gate_w2[l], gla_gate_b[l], gla_norm_w[l], ssm_conv_w[l],
                             ssm_conv_b[l], ssm_dt_bias[l], ssm_A_log[l], ssm_D[l], ssm_norm_w[l],
                             w_branch_a[l], w_branch_b[l], w_mix_out[l])
        x = x + swiglu(rms_norm(x, norm2_w[l]), w_ffn_in[l], w_ffn_out[l])
    return rms_norm(x, final_norm_w)
```

```python
import numpy as np
import concourse.bass as bass
import concourse.mybir as mybir
from concourse.bass_utils import run_bass_kernel_spmd

F32 = mybir.dt.float32
BF = mybir.dt.bfloat16
ALU = mybir.AluOpType
AF = mybir.ActivationFunctionType

D = 1024
SEQ = 4096
DEPTH = 4
NCORES = 8
EPS = 1e-6
FFN = 2816
WCH = 4096
WMAX = 5632
NWB = 3
NCHUNK = 32
NDMA = 12
NPDMA = 6

W_IN_OFF = dict(q=0, k=512, v=1024, r=2048, glr=3072, z=3088, xbc=5136, dt=8208, ga=8240, gb=9264)


def _stream_spec():
    spec = [("SM", "sm", None), ("Q", "in", W_IN_OFF["q"]), ("K", "in", W_IN_OFF["k"])]
    spec += [("V%d" % i, "in", W_IN_OFF["v"] + 512 * i) for i in range(2)]
    spec += [("R%d" % i, "in", W_IN_OFF["r"] + 512 * i) for i in range(2)]
    spec += [("GA%d" % i, "in", W_IN_OFF["ga"] + 512 * i) for i in range(2)]
    spec += [("WYA%d" % i, "ya", 512 * i) for i in range(2)]
    spec += [("B", "in", W_IN_OFF["xbc"] + 2048), ("C", "in", W_IN_OFF["xbc"] + 2560)]
    for g in range(4):
        spec += [("XS%d" % g, "in", W_IN_OFF["xbc"] + 512 * g), ("Z%d" % g, "in", W_IN_OFF["z"] + 512 * g)]
    spec += [("GB%d" % i, "in", W_IN_OFF["gb"] + 512 * i) for i in range(2)]
    spec += [("WYB%d" % i, "yb", 256 * i) for i in range(4)]
    spec += [("WO%d" % i, "out", 512 * i) for i in range(2)]
    spec += [("FI%d" % i, "fi", i) for i in range(11)]
    spec += [("FO%d" % i, "fo", 256 * i) for i in range(4)]
    return spec


def _blk_elems(kind):
    return {"sm": 8 * 48, "in": 4096, "ya": 4096, "yb": 4096, "out": 4096, "fi": 4096, "fo": 22 * 256}[kind]


STREAM = _stream_spec()
STREAM_OFF = {}
_o = 0
for _n, _k, _a in STREAM:
    STREAM_OFF[_n] = (_o, _blk_elems(_k))
    _o += _blk_elems(_k)
NWL = _o


def _tile_k(mat, c0, nc_):
    K = mat.shape[0]
    return mat[:, c0:c0 + nc_].reshape(K // 128, 128, nc_).transpose(1, 0, 2).reshape(128, -1)


def host_weight_stream(w_in, w_ya, w_yb, w_out, w_fi, w_fo):
    out = np.empty((128, NWL), np.float32)
    for name, kind, a in STREAM:
        o, n = STREAM_OFF[name]
        if kind == "sm":
            m = np.concatenate([w_in[:, W_IN_OFF["glr"]:W_IN_OFF["glr"] + 16],
                                w_in[:, W_IN_OFF["dt"]:W_IN_OFF["dt"] + 32]], axis=1)
            blk = _tile_k(m, 0, 48)
        elif kind == "in":
            blk = _tile_k(w_in, a, 512)
        elif kind == "ya":
            blk = _tile_k(w_ya, a, 512)
        elif kind == "yb":
            blk = _tile_k(w_yb, a, 256)
        elif kind == "out":
            blk = _tile_k(w_out, a, 512)
        elif kind == "fi":
            m = np.concatenate([w_fi[:, a * 256:(a + 1) * 256], w_fi[:, FFN + a * 256:FFN + (a + 1) * 256]], axis=1)
            blk = _tile_k(m, 0, 512)
        elif kind == "fo":
            blk = _tile_k(w_fo, a, 256)
        out[:, o:o + n] = blk
    return out


PAGE = 2048


class Sched:
    def __init__(self, nc):
        self.nc = nc
        self.ops = []
        self.pages = {}
        self.rdedupe = {}

    @staticmethod
    def region(ap):
        sp = str(ap.space)
        dsz = 2 if ap.dtype == BF else 4
        pairs = ap.ap
        off = int(ap.offset)
        if "DRAM" in sp:
            ext = sum((c - 1) * abs(s) for s, c in pairs) + 1
            return (ap.tensor.name, 0, 1, off * dsz, (off + ext) * dsz)
        rowb = 16384 if "PSUM" in sp else ARENA_BYTES
        rowel = rowb // dsz
        p0 = off // rowel
        f0 = off % rowel
        pc = pairs[0][1]
        ext = sum((c - 1) * abs(s) for s, c in pairs[1:]) + 1
        return (sp, p0, p0 + pc, f0 * dsz, (f0 + ext) * dsz)

    def _pages(self, r):
        pg = PAGE if r[0] in ("SB", "PSUM") else (1 << 20)
        return range(r[3] // pg, (r[4] - 1) // pg + 1)

    def add(self, eng, fn, reads, writes, dma=False):
        idx = len(self.ops)
        deps = set()
        rregs = [a if isinstance(a, tuple) else self.region(a) for a in reads]
        wregs = [a if isinstance(a, tuple) else self.region(a) for a in writes]
        for r in rregs:
            for pg in self._pages(r):
                for rec in self.pages.get((r[0], pg), ()):
                    if rec[6] and rec[7] and rec[1] < r[2] and r[1] < rec[2] and rec[3] < r[4] and r[3] < rec[4]:
                        deps.add(rec[5])
        for r in wregs:
            for pg in self._pages(r):
                lst = self.pages.get((r[0], pg))
                if not lst:
                    continue
                keep = []
                for rec in lst:
                    if not rec[7]:
                        continue
                    if rec[1] < r[2] and r[1] < rec[2] and rec[3] < r[4] and r[3] < rec[4]:
                        deps.add(rec[5])
                        if r[1] <= rec[1] and rec[2] <= r[2] and r[3] <= rec[3] and rec[4] <= r[4]:
                            rec[7] = False
                            continue
                    keep.append(rec)
                self.pages[(r[0], pg)] = keep
        for r in rregs:
            key = (eng, r)
            old = self.rdedupe.get(key)
            if old is not None and old[7] and eng != "sp" and not dma:
                old[5] = idx
                continue
            rec = [r[0], r[1], r[2], r[3], r[4], idx, False, True]
            self.rdedupe[key] = rec
            for pg in self._pages(r):
                self.pages.setdefault((r[0], pg), []).append(rec)
        for r in wregs:
            rec = [r[0], r[1], r[2], r[3], r[4], idx, True, True]
            for pg in self._pages(r):
                self.pages.setdefault((r[0], pg), []).append(rec)
        deps.discard(idx)
        self.ops.append([eng, fn, deps, False, dma])
        return idx

    def emit(self):
        nc = self.nc
        ops = self.ops
        engs = ["pe", "act", "dve", "pool", "sp"]
        NQ = {"sp": NDMA, "pool": NPDMA}
        isdma = [(o[0] == "sp") or (len(o) > 4 and o[4]) for o in ops]
        need = []
        for i, o in enumerate(ops):
            eng, deps = o[0], o[2]
            best = {}
            dmas = []
            for d in deps:
                de = ops[d][0]
                if isdma[d]:
                    dmas.append(d)
                else:
                    if de == "pe" and eng == "pe":
                        continue
                    if d > best.get(de, -1):
                        best[de] = d
            for d in best.values():
                ops[d][3] = True
            need.append((best, sorted(dmas)))
        cnt = {}
        run = {e: 0 for e in engs}
        dma_slot = {}
        dma_val = {}
        slot_run = {q: [0] * NQ[q] for q in NQ}
        ndma = {q: 0 for q in NQ}
        for i, o in enumerate(ops):
            eng, sig = o[0], o[3]
            if isdma[i]:
                s = ndma[eng] % NQ[eng]
                slot_run[eng][s] += 16
                dma_slot[i] = (eng, s)
                dma_val[i] = slot_run[eng][s]
                ndma[eng] += 1
            elif sig:
                run[eng] += 1
                cnt[i] = run[eng]
        import contextlib
        with contextlib.ExitStack() as es:
            sems = {e: es.enter_context(nc.semaphore("s_" + e)) for e in ["pe", "act", "dve", "pool"]}
            dsems = {(q, k): es.enter_context(nc.semaphore("s_%sdma%d" % (q, k))) for q in NQ for k in range(NQ[q])}
            block = es.enter_context(nc.Block())

            def stream(eng):
                def f(e):
                    seen = {x: 0 for x in ["pe", "act", "dve", "pool"]}
                    seen_d = {k: 0 for k in dsems}
                    for i, o in enumerate(ops):
                        en, fn, sig = o[0], o[1], o[3]
                        if en != eng:
                            continue
                        best, dmas = need[i]
                        for de, d in best.items():
                            c = cnt[d]
                            if c > seen[de]:
                                e.wait_ge(sems[de], c)
                                seen[de] = c
                        for d in dmas:
                            s, v = dma_slot[d], dma_val[d]
                            if v > seen_d[s]:
                                e.wait_ge(dsems[s], v)
                                seen_d[s] = v
                        if isdma[i]:
                            s = dma_slot[i]
                            prev = dma_val[i] - 16
                            if prev > seen_d[s]:
                                e.wait_ge(dsems[s], prev)
                                seen_d[s] = prev
                            fn(e).then_inc(dsems[s], 16)
                        else:
                            ins = fn(e)
                            if sig:
                                ins.then_inc(sems[en], 1)
                    if eng in NQ:
                        for k in range(NQ[eng]):
                            if slot_run[eng][k] > seen_d[(eng, k)]:
                                e.wait_ge(dsems[(eng, k)], slot_run[eng][k])
                return f

            block.tensor(stream("pe"))
            block.scalar(stream("act"))
            block.vector(stream("dve"))
            block.gpsimd(stream("pool"))
            block.sync(stream("sp"))


ARENA_BYTES = 206 * 1024


class Builder:
    def __init__(self, n_layers=DEPTH, n_tiles=8, TS=4, final=True, x_from_scratch=False):
        self.L = n_layers
        self.NT = n_tiles
        self.TS = TS
        self.TT = TS * 128
        self.final = final
        nc = bass.Bass("TRN2", target_bir_lowering=False)
        self.nc = nc
        self.S = Sched(nc)
        T = n_tiles * self.TT
        self.T = T
        self.x_in = nc.dram_tensor("x", [T, D], F32, kind="ExternalInput").ap()
        self.y_out = nc.dram_tensor("y", [T, D], F32, kind="ExternalOutput").ap()
        self.wsrc = nc.dram_tensor("wsrc", [n_layers, 128, NWL], F32, kind="ExternalInput").ap()
        self.cst = nc.dram_tensor("cst", [128, 3, 128], F32, kind="ExternalInput").ap()
        self.NPF = 8 + 8 + 2 + 96 + 24 + 16
        self.pf = nc.dram_tensor("pf", [n_layers, 128, self.NPF], F32, kind="ExternalInput").ap()
        self.NPR = 512 + 32 + 32 + 32
        self.pr = nc.dram_tensor("pr", [n_layers, 1, self.NPR], F32, kind="ExternalInput").ap()
        self.w2 = nc.dram_tensor("w2", [n_layers, 16, 512], F32, kind="ExternalInput").ap()
        self.fnw = nc.dram_tensor("fnw", [1, D], F32, kind="ExternalInput").ap()
        self.wbf = nc.dram_tensor("wbf", [n_layers, 128, NWL], BF, kind="Internal").ap()
        self.xscr = nc.dram_tensor("xscr", [T, D], F32, kind="Internal").ap()
        self.arena = nc.alloc_sbuf_tensor("arena", [128, ARENA_BYTES // 2], BF)
        self.pst = nc.alloc_psum_tensor("ps", [128, 4096], F32)
        self.top = 0
        self._rotc = {}
        self.rr = 0
        self.wq = []
        self.wnext = 0
        self.wissued = 0

    def alloc(self, nel, dt, parts=128, shape=None):
        dsz = 2 if dt == BF else 4
        nb = (nel * dsz + 63) // 64 * 64
        off = self.top
        self.top += nb
        assert self.top <= ARENA_BYTES, ("arena overflow", self.top)
        v = self.arena[0:parts, off // 2: off // 2 + nel * dsz // 2]
        if dt == F32:
            v = v.bitcast(F32)
        return v

    def rot(self, name):
        lst = getattr(self, name + "_l")
        k = self._rotc.get(name, 0)
        self._rotc[name] = k + 1
        v = lst[k % len(lst)]
        setattr(self, name, v)
        return v

    def alloc2(self, name, nel, dt, n=2):
        setattr(self, name + "_l", [self.alloc(nel, dt) for _ in range(n)])
        setattr(self, name, getattr(self, name + "_l")[0])

    def bank(self, b, dt=F32):
        v = self.pst[:, b * 512:(b + 1) * 512]
        if dt == BF:
            v = v.bitcast(BF)
        return v

    def nb(self):
        b = self.rr % 4
        self.rr += 1
        return b

    def mm(self, out, lhsT, rhs, start=True, stop=True):
        self.S.add("pe", lambda e: e.matmul(out, lhsT, rhs, start=start, stop=stop), [lhsT, rhs], [out])

    def tr(self, out, in_, ident):
        self.S.add("pe", lambda e: e.transpose(out, in_, ident), [in_, ident], [out])

    def act(self, out, in_, func, bias=None, scale=None, accum=None):
        kw = {}
        rd = [in_]
        wr = [out]
        if bias is not None:
            kw["bias"] = bias
            if not isinstance(bias, float):
                rd.append(bias)
        if scale is not None:
            kw["scale"] = scale
            if not isinstance(scale, float):
                rd.append(scale)
        if accum is not None:
            kw["accum_out"] = accum
            wr.append(accum)
        self.S.add("act", lambda e: e.activation(out, in_, func, **kw), rd, wr)

    def tt(self, eng, out, in0, in1, op):
        self.S.add(eng, lambda e: e.tensor_tensor(out, in0, in1, op), [in0, in1], [out])

    def ts(self, eng, out, in0, s1, s2, op0, op1=None, accum=None):
        rd = [in0] + [s for s in (s1, s2) if s is not None and not isinstance(s, float)]
        wr = [out] + ([accum] if accum is not None else [])
        kw = {}
        if op1 is not None:
            kw["op1"] = op1
        if accum is not None:
            kw["accum_out"] = accum
        self.S.add(eng, lambda e: e.tensor_scalar(out, in0, s1, s2, op0, **kw), rd, wr)

    def stt(self, eng, out, in0, sc, in1, op0, op1):
        rd = [in0, in1] + ([] if isinstance(sc, float) else [sc])
        self.S.add(eng, lambda e: e.scalar_tensor_tensor(out, in0, sc, in1, op0, op1), rd, [out])

    def rstd(self, out, ss, n):
        eps = self.EPSC[0:out.shape[0], 0:1]
        self.S.add("act", lambda e: e.activation(out, ss, AF.Ln, bias=eps, scale=1.0 / n), [ss, eps], [out])
        self.S.add("act", lambda e: e.activation(out, out, AF.Exp, scale=-0.5), [out], [out])

    def cp(self, eng, out, in_):
        if eng == "act":
            self.S.add("act", lambda e: e.activation(out, in_, AF.Copy), [in_], [out])
        else:
            self.S.add(eng, lambda e: e.tensor_copy(out, in_), [in_], [out])

    def ms(self, eng, out, val):
        self.S.add(eng, lambda e: e.memset(out, val), [], [out])

    def dma(self, out, in_, rreg=None, wreg=None, eng="sp"):
        rd = [] if in_.tensor.name in ("x", "wsrc", "cst", "pf", "pr", "w2", "fnw") else [in_]
        if rreg is not None:
            rd = [rreg]
        wr = [out] if wreg is None else [wreg]
        self.S.add(eng, lambda e: e.dma_start(out=out, in_=in_), rd, wr, dma=True)

    @staticmethod
    def v3(ap, a, b):
        return ap.rearrange("p (a b) -> p a b", a=a, b=b)

    @staticmethod
    def bcast(ap, pairs):
        return bass.AP(ap.tensor, ap.offset, [list(ap.ap[0])] + [list(p) for p in pairs])

    def plan_weights(self):
        q = []
        for l in range(self.L):
            for t in range(self.NT):
                for name, kind, a in STREAM:
                    o, n = STREAM_OFF[name]
                    q.append((l, o, n, name))
        self.wq = q

    def _issue_w(self):
        i = self.wissued
        l, o, n, name = self.wq[i]
        buf = self.wbuf[i % NWB]
        self.dma(buf[:, 0:n], self.wbf[l, :, o:o + n], rreg=("wbf", l, l + 1, o * 2, (o + n) * 2))
        self.wissued += 1

    def next_w(self, expect):
        i = self.wnext
        l, o, n, name = self.wq[i]
        assert name == expect, (name, expect)
        while self.wissued < min(len(self.wq), i + NWB):
            self._issue_w()
        self.wnext += 1
        return self.wbuf[i % NWB][:, 0:n]

    def build(self):
        nc = self.nc
        TS, TT = self.TS, self.TT
        v3 = self.v3
        cst = self.alloc(3 * 128, F32)
        c3 = v3(cst, 3, 128)
        IDf, ULE, UGT = c3[:, 0, :], c3[:, 1, :], c3[:, 2, :]
        self.dma(cst, self.cst.rearrange("p a b -> p (a b)"))
        IDb = self.alloc(128, BF)
        self.cp("dve", IDb, IDf)
        ONESf = self.alloc(128, F32)
        self.ms("pool", ONESf, 1.0)
        ONESb = self.alloc(128, BF)
        self.ms("pool", ONESb, 1.0)
        self.IDb, self.ULE, self.UGT, self.ONESf, self.ONESb = IDb, ULE, UGT, ONESf, ONESb
        self.EPSC = self.alloc(16, F32)
        self.ms("pool", self.EPSC, EPS)
        PF = self.alloc(self.NPF, F32)
        self.n1T, self.n2T, self.gnT = PF[:, 0:8], PF[:, 8:16], PF[:, 16:18]
        self.cwT = v3(PF[:, 18:114], 24, 4)
        self.cbT, self.snT = PF[:, 114:138], PF[:, 138:154]
        self.PF = PF
        PRf = self.alloc(self.NPR, F32, parts=1)
        PRb = self.alloc(self.NPR, BF, parts=1)
        self.PRf, self.PRb = PRf, PRb
        ABC = self.alloc(64, F32)
        self.ABC = ABC
        self.AROW = self.alloc(32, F32)
        W2f = self.alloc(512, F32, parts=16)
        self.W2f = W2f
        self.W2b = self.alloc(512, BF, parts=16)
        self.DI = self.alloc(32 * 128, BF)
        self.FNW = self.alloc(D, F32)
        self.dma(self.FNW, self.fnw.partition_broadcast(128).rearrange("p a b -> p (a b)"))
        self.Sg = self.alloc(1024, F32)
        self.Sgb = self.alloc(1024, BF)
        self.SS = self.alloc(2048, F32)
        self.HIST = self.alloc(24 * 3, F32)
        self.X = self.alloc(TS * D, F32)
        self.hT = self.alloc(8 * TT, BF)
        self.JUNK = self.alloc(1024, BF)
        self.st = self.alloc(64, F32)
        self.wbuf = [self.alloc(WMAX, BF) for _ in range(NWB)]
        self.YAG = self.alloc(8 * TT, BF)
        base = self.top
        self.convert_layer(0, 0, NCHUNK)
        self.top = base
        self.alloc_phases()
        self.plan_weights()
        for l in range(self.L):
            self.layer_setup(l)
            for t in range(self.NT):
                if l + 1 < self.L:
                    per = (NCHUNK + self.NT - 1) // self.NT
                    self.convert_layer(l + 1, t * per, min(NCHUNK, (t + 1) * per))
                self.tile(l, t)
        self.S.emit()
        return nc

    def alloc_phases(self):
        TS, TT = self.TS, self.TT
        base = self.top
        self.GLRT = self.alloc(TT, BF, parts=16)
        self.alloc2("LG", 512, F32)
        self.alloc2("E1", 512, F32)
        self.EBT = self.alloc(4 * TT, F32)
        self.ENBT = self.alloc(4 * TT, F32)
        self.ED = self.alloc(TS * 512, F32)
        self.QGT = self.alloc(4 * TT, BF)
        self.KGT = self.alloc(4 * TT, BF)
        self.KD = self.alloc(TS * 512, BF)
        self.V = self.alloc(TS * 1024, BF)
        self.SRW = self.alloc(8 * TT, BF)
        self.OGT = self.alloc(8 * TT, BF)
        self.alloc2("ATM", 512, BF)
        self.alloc2("ON", 1024, BF)
        self.alloc2("SGT", TT, F32)
        self.alloc2("OF", 1024, F32)
        gla_top = self.top
        self.top = base
        self.DT = self.alloc(TS * 32, F32)
        self.AA = self.alloc(TS * 32, F32)
        self.ECUM = self.alloc(TS * 32, F32)
        self.DEC = self.alloc(TS * 32, F32)
        self.ECL = self.alloc(TS * 32, F32)
        self.BT = self.alloc(4 * TT, BF)
        self.CT = self.alloc(4 * TT, BF)
        self.BTOK = self.alloc(TS * 512, BF)
        self.CBM = self.alloc(TS * 512, BF)
        off_xr = self.top
        self.XR = self.alloc(4 * (TT + 4), F32)
        self.ACC = self.alloc(4 * TT, F32)
        self.XC = self.alloc(4 * TT, BF)
        self.XS = self.alloc(TS * 512, BF)
        self.XDT = self.alloc(TS * 512, BF)
        self.SZ = self.alloc(TS * 512, BF)
        self.alloc2("AE", 512, F32)
        self.alloc2("EX", 512, F32, n=4)
        self.MTA = self.alloc(TS * 1024, BF)
        self.SSBV = self.alloc(TS * 512, BF)
        self.alloc2("YO", 512, F32)
        self.alloc2("YN", 512, BF)
        self.alloc2("XDD", 512, BF)
        self.YNT = self.alloc(16 * TT, BF)
        self.SGB = self.arena[:, off_xr // 2: off_xr // 2 + 8 * TT]
        ssd_top = self.top
        self.top = base
        self.ACTT = self.alloc(22 * TT, BF)
        self.SG = [self.alloc(TT, F32) for _ in range(2)]
        ffn_top = self.top
        self.top = max(gla_top, ssd_top, ffn_top)
        self.phase_tops = (base, gla_top, ssd_top, ffn_top)

    def convert_layer(self, l, c_lo, c_hi):
        csz = (NWL + NCHUNK - 1) // NCHUNK
        for c in range(c_lo, c_hi):
            c0 = c * csz
            n = min(csz, NWL - c0)
            if n <= 0:
                continue
            self.dma(self.wbf[l, :, c0:c0 + n], self.wsrc[l, :, c0:c0 + n],
                     wreg=("wbf", l, l + 1, c0 * 2, (c0 + n) * 2), eng="pool")

    def layer_setup(self, l):
        v3 = self.v3
        self.dma(self.PF, self.pf[l])
        self.dma(self.PRf, self.pr[l])
        self.cp("pool", self.PRb, self.PRf)
        self.dma(self.W2f, self.w2[l])
        self.cp("pool", self.W2b, self.W2f)
        self.dma(self.ABC, self.pr[l, :, 544:608].partition_broadcast(128).rearrange("p a b -> p (a b)"))
        self.act(self.AROW, self.ABC[:, 0:32], AF.Exp)
        self.ts("dve", self.AROW, self.AROW, -1.0, None, ALU.mult)
        DI3 = v3(self.DI, 32, 128)
        idb = self.bcast(self.IDb, [[0, 32], [1, 128]])
        dsk = self.bcast(self.ABC[:, 32:64], [[1, 32], [0, 128]])
        self.tt("pool", DI3, idb, dsk, ALU.mult)
        self.ms("pool", self.Sg, 0.0)
        self.ms("pool", self.Sgb, 0.0)
        self.ms("pool", self.SS, 0.0)
        self.ms("pool", self.HIST, 0.0)

    def norm_to_hT(self, nT):
        TS, TT = self.TS, self.TT
        X3 = self.v3(self.X, TS, D)
        hT3 = self.v3(self.hT, 8, TT)
        ss = self.st[:, 0:TS]
        rs = self.st[:, 8:8 + TS]
        for s in range(TS):
            self.act(self.JUNK, X3[:, s, :], AF.Square, accum=ss[:, s:s + 1])
        self.rstd(rs, ss, D)
        for s in range(TS):
            self.rot("ON")
            self.act(self.ON, X3[:, s, :], AF.Copy, scale=rs[:, s:s + 1])
            b = 4 + (s % 2)
            tb = self.bank(b, BF)
            for c in range(8):
                self.tr(tb[:, c * 128:(c + 1) * 128], self.ON[:, c * 128:(c + 1) * 128], self.IDb)
            self.tt("dve", hT3[:, :, s * 128:(s + 1) * 128], self.v3(tb, 8, 128),
                    self.bcast(nT, [[1, 8], [0, 128]]), ALU.mult)

    def proj_fm(self, W3, c0, rhs3, nk, bank):
        out = self.bank(bank)[:, 0:self.TT]
        for k in range(nk):
            self.mm(out, W3[:, k, c0:c0 + 128], rhs3[:, k, :], start=(k == 0), stop=(k == nk - 1))
        return out

    def proj_tm(self, lhs3, s, W3, ncols, nk, bank):
        out = self.bank(bank)[:, 0:ncols]
        for k in range(nk):
            self.mm(out, lhs3[:, k, s * 128:(s + 1) * 128], W3[:, k, 0:ncols], start=(k == 0), stop=(k == nk - 1))
        return out

    def tile(self, l, t):
        TS, TT = self.TS, self.TT
        v3 = self.v3
        bc = self.bcast
        X3 = v3(self.X, TS, D)
        hT3 = v3(self.hT, 8, TT)
        src = self.x_in if l == 0 else self.xscr
        self.dma(X3, src[t * TT:(t + 1) * TT, :].rearrange("(s p) d -> p s d", p=128))
        import os
        stage = int(os.environ.get("KSTAGE", "99"))
        if stage <= 0:
            self.dma(self.y_out[t * TT:(t + 1) * TT, :].rearrange("(s p) d -> p s d", p=128), X3)
            return
        self.norm_to_hT(self.n1T)
        def stop(n):
            if stage <= n:
                self.dma(self.y_out[t * TT:(t + 1) * TT, :].rearrange("(s p) d -> p s d", p=128), X3)
                return True
            return False
        if stop(1):
            return

        W = v3(self.next_w("SM"), 8, 48)
        b = self.nb()
        o = self.bank(b)[0:16, 0:TT]
        for k in range(8):
            self.mm(o, W[:, k, 0:16], hT3[:, k, :], start=(k == 0), stop=(k == 7))
        self.cp("act", self.GLRT, o)
        DT3 = v3(self.DTp, TS, 32)
        b = self.nb()
        dtb = self.bank(b)
        for s in range(TS):
            o = dtb[:, s * 32:(s + 1) * 32]
            for k in range(8):
                self.mm(o, hT3[:, k, s * 128:(s + 1) * 128], W[:, k, 16:48], start=(k == 0), stop=False)
            self.mm(o, self.ONESb[0:1, 0:128], self.PRb[0:1, 512:544], start=False, stop=True)
        self.act(self.DTe, dtb[:, 0:TS * 32], AF.Exp)
        self.act(self.DTp, self.DTe, AF.Ln, bias=1.0)

        if stop(2):
            return
        EBT3, ENBT3, ED3 = v3(self.EBT, 4, TT), v3(self.ENBT, 4, TT), v3(self.ED, TS, 512)
        for s in range(TS):
            sc = slice(s * 128, (s + 1) * 128)
            self.rot("E1")
            self.rot("LG")
            b = self.nb()
            lg = self.bank(b)
            self.mm(lg, self.GLRT[0:16, sc], self.W2b[0:16, :], start=True, stop=False)
            self.mm(lg, self.ONESb[0:1, 0:128], self.PRb[0:1, 0:512], start=False, stop=True)
            self.act(self.E1, lg, AF.Exp, scale=-1.0)
            self.act(self.LG, self.E1, AF.Ln, bias=1.0)
            b = self.nb()
            bt = self.bank(b)
            for h in range(4):
                self.mm(bt[:, h * 128:(h + 1) * 128], self.LG[:, h * 128:(h + 1) * 128], self.ULE)
            self.act(EBT3[:, :, sc], v3(bt, 4, 128), AF.Exp, scale=-1.0 / 16)
            self.act(ENBT3[:, :, sc], v3(bt, 4, 128), AF.Exp, scale=1.0 / 16)
            b = self.nb()
            dr = self.bank(b)
            self.mm(dr, self.UGT, self.LG)
            self.act(ED3[:, s, :], dr, AF.Exp, scale=-1.0 / 16)

        if stop(3):
            return
        QGT3, KGT3, KD3, V3_ = v3(self.QGT, 4, TT), v3(self.KGT, 4, TT), v3(self.KD, TS, 512), v3(self.V, TS, 1024)
        W = v3(self.next_w("Q"), 8, 512)
        for h in range(4):
            o = self.proj_fm(W, h * 128, hT3, 8, self.nb())
            self.stt("dve", QGT3[:, h, :], o, 128.0 ** -0.5, EBT3[:, h, :], ALU.mult, ALU.mult)
        W = v3(self.next_w("K"), 8, 512)
        for h in range(4):
            o = self.proj_fm(W, h * 128, hT3, 8, self.nb())
            self.tt("dve", KGT3[:, h, :], o, ENBT3[:, h, :], ALU.mult)
        for s in range(TS):
            o = self.proj_tm(hT3, s, W, 512, 8, self.nb())
            self.tt("dve", KD3[:, s, :], o, ED3[:, s, :], ALU.mult)
        for cb in range(2):
            W = v3(self.next_w("V%d" % cb), 8, 512)
            for s in range(TS):
                o = self.proj_tm(hT3, s, W, 512, 8, self.nb())
                self.cp("act", V3_[:, s, cb * 512:(cb + 1) * 512], o)
        SRW3 = v3(self.SRW, 8, TT)
        for cb in range(2):
            W = v3(self.next_w("R%d" % cb), 8, 512)
            for fb in range(4):
                j = cb * 4 + fb
                o = self.proj_fm(W, fb * 128, hT3, 8, self.nb())
                self.rot("SGT")
                self.act(self.SGT, o, AF.Silu)
                self.ts("dve", SRW3[:, j, :], self.SGT, self.gnT[:, j % 2:j % 2 + 1], None, ALU.mult)

        if stop(4):
            return
        S3, Sb3 = v3(self.Sg, 4, 256), v3(self.Sgb, 4, 256)
        OGT3 = v3(self.OGT, 8, TT)
        for s in range(TS):
            sc = slice(s * 128, (s + 1) * 128)
            ATM3 = v3(self.rot("ATM"), 4, 128)
            self.rot("OF")
            self.rot("ON")
            oss = self.st[:, 16 + 8 * (s % 2):20 + 8 * (s % 2)]
            ors = self.st[:, 20 + 8 * (s % 2):24 + 8 * (s % 2)]
            b = self.nb()
            at = self.bank(b)
            for h in range(4):
                self.mm(at[:, h * 128:(h + 1) * 128], KGT3[:, h, sc], QGT3[:, h, sc])
            self.tt("dve", ATM3, v3(at, 4, 128), bc(self.ULE, [[0, 4], [1, 128]]), ALU.mult)
            sub = int(os.environ.get("KSUB", "99"))
            if sub <= 0:
                continue
            ob = [self.bank(4), self.bank(5)]
            db = [self.bank(6), self.bank(7)]
            for h in range(4):
                o = ob[h // 2][:, (h % 2) * 256:(h % 2 + 1) * 256]
                self.mm(o, ATM3[:, h, :], V3_[:, s, h * 256:(h + 1) * 256], start=True, stop=False)
                self.mm(o, QGT3[:, h, sc], Sb3[:, h, :], start=False, stop=True)
            if sub <= 1:
                continue
            for h in range(4):
                d = db[h // 2][:, (h % 2) * 256:(h % 2 + 1) * 256]
                self.mm(d, KD3[:, s, h * 128:(h + 1) * 128], V3_[:, s, h * 256:(h + 1) * 256])
            if sub <= 2:
                continue
            for h in range(4):
                d = db[h // 2][:, (h % 2) * 256:(h % 2 + 1) * 256]
                self.stt("dve", S3[:, h, :], S3[:, h, :], EBT3[:, h, s * 128 + 127:s * 128 + 128], d, ALU.mult, ALU.add)
                self.cp("act", Sb3[:, h, :], S3[:, h, :])
            if sub <= 3:
                continue
            for h in range(4):
                o = ob[h // 2][:, (h % 2) * 256:(h % 2 + 1) * 256]
                self.cp("act", self.OF[:, h * 256:(h + 1) * 256], o)
                self.act(self.JUNK[:, 0:256], self.OF[:, h * 256:(h + 1) * 256], AF.Square, accum=oss[:, h:h + 1])
            sub2 = int(os.environ.get("KSUB2", "99"))
            if sub2 <= 0:
                continue
            self.rstd(ors, oss, 256)
            if sub2 <= 1:
                continue
            for h in range(4):
                o = ob[h // 2][:, (h % 2) * 256:(h % 2 + 1) * 256]
                self.act(self.ON[:, h * 256:(h + 1) * 256], self.OF[:, h * 256:(h + 1) * 256], AF.Copy, scale=ors[:, h:h + 1])
            if sub <= 4:
                continue
            b = self.nb()
            tb = self.bank(b, BF)
            for j in range(8):
                self.tr(tb[:, j * 128:(j + 1) * 128], self.ON[:, j * 128:(j + 1) * 128], self.IDb)
            self.tt("dve", OGT3[:, :, sc], v3(tb, 8, 128), SRW3[:, :, sc], ALU.mult)

        if stop(5):
            return
        YAG3 = v3(self.YAG, 8, TT)
        for cb in range(2):
            W = v3(self.next_w("GA%d" % cb), 8, 512)
            for fb in range(4):
                o = self.proj_fm(W, fb * 128, hT3, 8, self.nb())
                self.act(YAG3[:, cb * 4 + fb, :], o, AF.Sigmoid)
        for cb in range(2):
            W = v3(self.next_w("WYA%d" % cb), 8, 512)
            for fb in range(4):
                o = self.proj_fm(W, fb * 128, OGT3, 8, self.nb())
                self.tt("dve", YAG3[:, cb * 4 + fb, :], o, YAG3[:, cb * 4 + fb, :], ALU.mult)

        if stop(6):
            return
        self.ssd(l, t)
        if stop(7):
            return

        SGB3 = v3(self.SGB, 8, TT)
        YNT3 = v3(self.YNT, 16, TT)
        for cb in range(2):
            W = v3(self.next_w("GB%d" % cb), 8, 512)
            for fb in range(4):
                o = self.proj_fm(W, fb * 128, hT3, 8, self.nb())
                self.act(SGB3[:, cb * 4 + fb, :], o, AF.Sigmoid)
        for cb in range(4):
            W = v3(self.next_w("WYB%d" % cb), 16, 256)
            for fb in range(2):
                j = cb * 2 + fb
                o = self.proj_fm(W, fb * 128, YNT3, 16, self.nb())
                self.tt("dve", SGB3[:, j, :], o, SGB3[:, j, :], ALU.mult)
                self.tt("dve", SGB3[:, j, :], SGB3[:, j, :], YAG3[:, j, :], ALU.add)
        for cb in range(2):
            W = v3(self.next_w("WO%d" % cb), 8, 512)
            for s in range(TS):
                o = self.proj_tm(SGB3, s, W, 512, 8, self.nb())
                xs_ = X3[:, s, cb * 512:(cb + 1) * 512]
                self.tt("dve", xs_, xs_, o, ALU.add)

        if stop(8):
            return
        self.norm_to_hT(self.n2T)
        ACTT3 = v3(self.ACTT, 22, TT)
        for i in range(11):
            W = v3(self.next_w("FI%d" % i), 8, 512)
            for q in range(2):
                og = self.proj_fm(W, q * 128, hT3, 8, self.nb())
                ou = self.proj_fm(W, 256 + q * 128, hT3, 8, self.nb())
                sg = self.SG[q]
                self.act(sg, og, AF.Silu)
                self.tt("dve", ACTT3[:, i * 2 + q, :], sg, ou, ALU.mult)
        for cb in range(4):
            W = v3(self.next_w("FO%d" % cb), 22, 256)
            for s in range(TS):
                o = self.proj_tm(ACTT3, s, W, 256, 22, self.nb())
                xs_ = X3[:, s, cb * 256:(cb + 1) * 256]
                self.tt("dve", xs_, xs_, o, ALU.add)

        rows = slice(t * TT, (t + 1) * TT)
        if l == self.L - 1 and self.final:
            ss = self.st[:, 0:TS]
            rs = self.st[:, 8:8 + TS]
            for s in range(TS):
                self.act(self.JUNK, X3[:, s, :], AF.Square, accum=ss[:, s:s + 1])
            self.rstd(rs, ss, D)
            for s in range(TS):
                self.stt("dve", X3[:, s, :], X3[:, s, :], rs[:, s:s + 1], self.FNW, ALU.mult, ALU.mult)
            self.dma(self.y_out[rows, :].rearrange("(s p) d -> p s d", p=128), X3)
        elif l == self.L - 1:
            self.dma(self.y_out[rows, :].rearrange("(s p) d -> p s d", p=128), X3)
        else:
            self.dma(self.xscr[rows, :].rearrange("(s p) d -> p s d", p=128), X3)

    def ssd(self, l, t):
        TS, TT = self.TS, self.TT
        v3 = self.v3
        bc = self.bcast
        hT3 = v3(self.hT, 8, TT)
        DT3, AA3 = v3(self.DT, TS, 32), v3(self.AA, TS, 32)
        self.cp("pool", self.DT, self.DTp)
        self.tt("dve", AA3, DT3, bc(self.AROW, [[0, TS], [1, 32]]), ALU.mult)
        b = self.nb()
        cb_ = self.bank(b)
        for s in range(TS):
            a_s = self.AA[:, s * 32:(s + 1) * 32]
            self.mm(cb_[:, s * 32:(s + 1) * 32], self.ULE, a_s)
            self.mm(cb_[:, 128 + s * 32:128 + (s + 1) * 32], self.UGT, a_s)
            self.mm(cb_[:, 256 + s * 32:256 + (s + 1) * 32], self.ONESf, a_s)
        self.act(self.ECUM, cb_[:, 0:TS * 32], AF.Exp)
        self.act(self.DEC, cb_[:, 128:128 + TS * 32], AF.Exp)
        self.act(self.ECL, cb_[:, 256:256 + TS * 32], AF.Exp)
        ECUM3, DEC3, ECL3 = v3(self.ECUM, TS, 32), v3(self.DEC, TS, 32), v3(self.ECL, TS, 32)

        import os
        kssd = int(os.environ.get("KSSD", "99"))
        if kssd <= 0:
            return
        XR3 = v3(self.XR, 4, TT + 4)
        ACC3 = v3(self.ACC, 4, TT)
        HIST3 = v3(self.HIST, 24, 3)

        def conv_proj(W, fb0):
            self.cp("pool", XR3[:, :, 0:3], HIST3[:, fb0:fb0 + 4, :])
            for fb in range(4):
                o = self.proj_fm(W, fb * 128, hT3, 8, self.nb())
                self.cp("act", XR3[:, fb, 3:3 + TT], o)
            self.cp("pool", HIST3[:, fb0:fb0 + 4, :], XR3[:, :, TT:TT + 3])

        def conv_apply(fb0, out3):
            for fb in range(4):
                cw = self.cwT[:, fb0 + fb, :]
                self.act(ACC3[:, fb, :], XR3[:, fb, 0:TT], AF.Copy, scale=cw[:, 0:1])
                for k in range(1, 4):
                    self.stt("dve", ACC3[:, fb, :], XR3[:, fb, k:k + TT], cw[:, k:k + 1], ACC3[:, fb, :], ALU.mult, ALU.add)
                self.act(out3[:, fb, :], ACC3[:, fb, :], AF.Silu, bias=self.cbT[:, fb0 + fb:fb0 + fb + 1])

        def conv_block(W, fb0, out3):
            conv_proj(W, fb0)
            conv_apply(fb0, out3)

        BT3, CT3 = v3(self.BT, 4, TT), v3(self.CT, 4, TT)
        conv_block(v3(self.next_w("B"), 8, 512), 16, BT3)
        conv_block(v3(self.next_w("C"), 8, 512), 20, CT3)
        if kssd <= 1:
            return
        BTOK3 = v3(self.BTOK, TS, 512)
        CBM = self.CBM
        for s in range(TS):
            sc = slice(s * 128, (s + 1) * 128)
            b = self.nb()
            tb = self.bank(b, BF)
            for g in range(4):
                self.tr(tb[:, g * 128:(g + 1) * 128], BT3[:, g, sc], self.IDb)
            self.cp("act", BTOK3[:, s, :], tb[:, 0:512])
            b = self.nb()
            cbk = self.bank(b)
            for g in range(4):
                self.mm(cbk[:, g * 128:(g + 1) * 128], BT3[:, g, sc], CT3[:, g, sc])
            self.tt("dve", v3(CBM[:, s * 512:(s + 1) * 512], 4, 128), v3(cbk, 4, 128),
                    bc(self.ULE, [[0, 4], [1, 128]]), ALU.mult)

        if kssd <= 2:
            return
        XC3 = v3(self.XC, 4, TT)
        XS3, XDT3, SZ3 = v3(self.XS, TS, 512), v3(self.XDT, TS, 512), v3(self.SZ, TS, 512)
        YNT3 = v3(self.YNT, 16, TT)
        DI3 = v3(self.DI, 32, 128)
        SS3 = v3(self.SS, 4, 512)
        yss = self.st[:, 24:25]
        yrs = self.st[:, 25:26]
        for g in range(4):
            conv_proj(v3(self.next_w("XS%d" % g), 8, 512), g * 4)
            W = v3(self.next_w("Z%d" % g), 8, 512)
            for s in range(TS):
                o = self.proj_tm(hT3, s, W, 512, 8, self.nb())
                self.act(SZ3[:, s, :], o, AF.Silu)
            conv_apply(g * 4, XC3)
            for s in range(TS):
                sc = slice(s * 128, (s + 1) * 128)
                tb = self.bank(4 + (s % 2), BF)
                for fb in range(4):
                    self.tr(tb[:, fb * 128:(fb + 1) * 128], XC3[:, fb, sc], self.IDb)
                self.cp("act", XS3[:, s, :], tb[:, 0:512])
                if os.environ.get("KNOXDT") != "1":
                    self.tt("dve", v3(XDT3[:, s, :], 8, 64), v3(XS3[:, s, :], 8, 64),
                            bc(DT3[:, s, g * 8:(g + 1) * 8], [[1, 8], [0, 64]]), ALU.mult)
            SSBV3 = v3(self.SSBV, TS, 512)
            MTA5 = self.MTA.rearrange("p (s h e i) -> p s h e i", s=TS, h=2, e=4, i=128)
            self.cp("act", SSBV3[:, 0, :], SS3[:, g, :])
            pend = []

            def flush_mt():
                for (ss_, half_, ex_) in pend:
                    cbm = CBM[:, ss_ * 512 + g * 128: ss_ * 512 + (g + 1) * 128]
                    self.tt("dve", MTA5[:, ss_, half_, :, :], v3(ex_, 4, 128), bc(cbm, [[0, 4], [1, 128]]), ALU.mult)
                del pend[:]

            for s in range(TS):
                self.rot("XDD")
                self.tt("dve", v3(self.XDD, 8, 64), v3(XDT3[:, s, :], 8, 64),
                        bc(DEC3[:, s, g * 8:(g + 1) * 8], [[1, 8], [0, 64]]), ALU.mult)
                ds = self.bank(self.nb())
                self.mm(ds, BTOK3[:, s, g * 128:(g + 1) * 128], self.XDD)
                newp = []
                for half in range(2):
                    e0 = g * 8 + half * 4
                    self.rot("AE")
                    self.rot("EX")
                    AE3 = v3(self.AE, 4, 128)
                    for ei in range(4):
                        self.act(AE3[:, ei, :], self.UGT, AF.Copy, scale=AA3[:, s, e0 + ei:e0 + ei + 1])
                    sb = self.bank(self.nb())
                    for ei in range(4):
                        self.mm(sb[:, ei * 128:(ei + 1) * 128], AE3[:, ei, :], self.ULE)
                    self.act(self.EX, sb, AF.Exp)
                    newp.append((s, half, self.EX))
                flush_mt()
                pend.extend(newp)
                self.tt("dve", v3(SS3[:, g, :], 8, 64), v3(SS3[:, g, :], 8, 64),
                        bc(ECL3[:, s, g * 8:(g + 1) * 8], [[1, 8], [0, 64]]), ALU.mult)
                self.tt("dve", SS3[:, g, :], SS3[:, g, :], ds, ALU.add)
                if s < TS - 1:
                    self.cp("act", SSBV3[:, s + 1, :], SS3[:, g, :])
            flush_mt()

            st_b2 = []
            st_b3 = []

            def run_b3():
                for (ss_, yn_) in st_b3:
                    tb = self.bank(4 + (ss_ % 2), BF)
                    for fb in range(4):
                        self.tr(tb[:, fb * 128:(fb + 1) * 128], yn_[:, fb * 128:(fb + 1) * 128], self.IDb)
                    self.tt("dve", YNT3[:, g * 4:(g + 1) * 4, ss_ * 128:(ss_ + 1) * 128], v3(tb[:, 0:512], 4, 128),
                            bc(self.snT[:, g * 4:(g + 1) * 4], [[1, 4], [0, 128]]), ALU.mult)
                del st_b3[:]

            def run_b2():
                for (ss_, yo_, yn_, yss_, yrs_) in st_b2:
                    self.rstd(yrs_, yss_, 512)
                    self.act(yn_, yo_, AF.Copy, scale=yrs_)
                    st_b3.append((ss_, yn_))
                del st_b2[:]

            for s in range(TS):
                sc = slice(s * 128, (s + 1) * 128)
                self.rot("YO")
                self.rot("YN")
                k3 = s % 3
                yss = self.st[:, 40 + k3 * 2:41 + k3 * 2]
                yrs = self.st[:, 41 + k3 * 2:42 + k3 * 2]
                yo = self.bank(self.nb())
                self.mm(yo, CT3[:, g, sc], SSBV3[:, s, :])
                self.cp("act", self.YO, yo)
                self.tt("dve", v3(self.YO, 8, 64), v3(self.YO, 8, 64),
                        bc(ECUM3[:, s, g * 8:(g + 1) * 8], [[1, 8], [0, 64]]), ALU.mult)
                yb = self.bank(6 + (s % 2))
                for half in range(2):
                    e0 = g * 8 + half * 4
                    for ei in range(4):
                        c0 = (half * 4 + ei) * 64
                        self.mm(yb[:, c0:c0 + 64], MTA5[:, s, half, ei, :], XDT3[:, s, c0:c0 + 64], start=True, stop=False)
                        self.mm(yb[:, c0:c0 + 64], DI3[:, e0 + ei, :], XS3[:, s, c0:c0 + 64], start=False, stop=True)
                run_b3()
                self.tt("dve", self.YO, yb, self.YO, ALU.add)
                self.tt("dve", self.YO, self.YO, SZ3[:, s, :], ALU.mult)
                self.act(self.JUNK[:, 0:512], self.YO, AF.Square, accum=yss)
                run_b2()
                st_b2.append((s, self.YO, self.YN, yss, yrs))
            run_b3()
            run_b2()
            run_b3()


def build_program(n_layers=DEPTH, n_tiles=8, TS=4, final=True):
    B = Builder(n_layers, n_tiles, TS, final)
    B.DTp = B.alloc(TS * 32, F32)
    B.DTe = B.alloc(TS * 32, F32)
    B.build()
    return B.nc


def host_consts():
    c = np.zeros((128, 3, 128), np.float32)
    j = np.arange(128)[:, None]
    i = np.arange(128)[None, :]
    c[:, 0, :] = (j == i)
    c[:, 1, :] = (j <= i)
    c[:, 2, :] = (j > i)
    return c


def host_params(inp, n_layers=DEPTH):
    fm = lambda v, nb: np.ascontiguousarray(v.reshape(nb, 128).T)
    pf, pr, ws = [], [], []
    for l in range(n_layers):
        cw = inp["ssm_conv_w"][l]
        cwT = np.ascontiguousarray(cw.reshape(4, 24, 128).transpose(2, 1, 0)).reshape(128, 96)
        pf.append(np.concatenate([fm(inp["norm1_w"][l], 8), fm(inp["norm2_w"][l], 8), fm(inp["gla_norm_w"][l], 2),
                                  cwT, fm(inp["ssm_conv_b"][l], 24), fm(inp["ssm_norm_w"][l], 16)], axis=1))
        pr.append(np.concatenate([inp["gla_gate_b"][l], inp["ssm_dt_bias"][l], inp["ssm_A_log"][l],
                                  inp["ssm_D"][l]])[None, :])
        ws.append(host_weight_stream(inp["w_in"][l], inp["w_branch_a"][l], inp["w_branch_b"][l],
                                     inp["w_mix_out"][l], inp["w_ffn_in"][l], inp["w_ffn_out"][l]))
    return (np.ascontiguousarray(np.stack(pf)).astype(np.float32), np.ascontiguousarray(np.stack(pr)).astype(np.float32),
            np.stack(ws))


_CACHE = {}


def kernel(**inputs):
    inp = {k: np.asarray(v) for k, v in inputs.items()}
    x = inp["x"]
    pf, pr, ws = host_params(inp)
    if "nc" not in _CACHE:
        _CACHE["nc"] = build_program()
    nc = _CACHE["nc"]
    shared = dict(wsrc=ws, cst=host_consts(), pf=pf, pr=pr, w2=np.ascontiguousarray(inp["gla_gate_w2"]),
                  fnw=np.ascontiguousarray(inp["final_norm_w"][None, :]))
    in_maps = [dict(shared, x=np.ascontiguousarray(x[c])) for c in range(NCORES)]
    res = run_bass_kernel_spmd(nc, in_maps, core_ids=list(range(NCORES)))
    return np.stack([np.asarray(r["y"]) for r in res.results]).astype(np.float32)
```

```python
import numpy as np
import concourse.bass as bass
import concourse.mybir as mybir
from concourse.bass_utils import run_bass_kernel_spmd

F32 = mybir.dt.float32
BF = mybir.dt.bfloat16
ALU = mybir.AluOpType
AF = mybir.ActivationFunctionType

D = 1024
SEQ = 4096
DEPTH = 4
NCORES = 8
EPS = 1e-6
FFN = 2816
WCH = 4096
WMAX = 5632
NWB = 3
NCHUNK = 32
NDMA = 12
NPDMA = 6

W_IN_OFF = dict(q=0, k=512, v=1024, r=2048, glr=3072, z=3088, xbc=5136, dt=8208, ga=8240, gb=9264)


def _stream_spec():
    spec = [("SM", "sm", None), ("Q", "in", W_IN_OFF["q"]), ("K", "in", W_IN_OFF["k"])]
    spec += [("V%d" % i, "in", W_IN_OFF["v"] + 512 * i) for i in range(2)]
    spec += [("R%d" % i, "in", W_IN_OFF["r"] + 512 * i) for i in range(2)]
    spec += [("GA%d" % i, "in", W_IN_OFF["ga"] + 512 * i) for i in range(2)]
    spec += [("WYA%d" % i, "ya", 512 * i) for i in range(2)]
    spec += [("B", "in", W_IN_OFF["xbc"] + 2048), ("C", "in", W_IN_OFF["xbc"] + 2560)]
    for g in range(4):
        spec += [("XS%d" % g, "in", W_IN_OFF["xbc"] + 512 * g), ("Z%d" % g, "in", W_IN_OFF["z"] + 512 * g)]
    spec += [("GB%d" % i, "in", W_IN_OFF["gb"] + 512 * i) for i in range(2)]
    spec += [("WYB%d" % i, "yb", 256 * i) for i in range(4)]
    spec += [("WO%d" % i, "out", 512 * i) for i in range(2)]
    spec += [("FI%d" % i, "fi", i) for i in range(11)]
    spec += [("FO%d" % i, "fo", 256 * i) for i in range(4)]
    return spec


def _blk_elems(kind):
    return {"sm": 8 * 48, "in": 4096, "ya": 4096, "yb": 4096, "out": 4096, "fi": 4096, "fo": 22 * 256}[kind]


STREAM = _stream_spec()
STREAM_OFF = {}
_o = 0
for _n, _k, _a in STREAM:
    STREAM_OFF[_n] = (_o, _blk_elems(_k))
    _o += _blk_elems(_k)
NWL = _o


def _tile_k(mat, c0, nc_):
    K = mat.shape[0]
    return mat[:, c0:c0 + nc_].reshape(K // 128, 128, nc_).transpose(1, 0, 2).reshape(128, -1)


def host_weight_stream(w_in, w_ya, w_yb, w_out, w_fi, w_fo):
    out = np.empty((128, NWL), np.float32)
    for name, kind, a in STREAM:
        o, n = STREAM_OFF[name]
        if kind == "sm":
            m = np.concatenate([w_in[:, W_IN_OFF["glr"]:W_IN_OFF["glr"] + 16],
                                w_in[:, W_IN_OFF["dt"]:W_IN_OFF["dt"] + 32]], axis=1)
            blk = _tile_k(m, 0, 48)
        elif kind == "in":
            blk = _tile_k(w_in, a, 512)
        elif kind == "ya":
            blk = _tile_k(w_ya, a, 512)
        elif kind == "yb":
            blk = _tile_k(w_yb, a, 256)
        elif kind == "out":
            blk = _tile_k(w_out, a, 512)
        elif kind == "fi":
            m = np.concatenate([w_fi[:, a * 256:(a + 1) * 256], w_fi[:, FFN + a * 256:FFN + (a + 1) * 256]], axis=1)
            blk = _tile_k(m, 0, 512)
        elif kind == "fo":
            blk = _tile_k(w_fo, a, 256)
        out[:, o:o + n] = blk
    return out


PAGE = 2048


class Sched:
    def __init__(self, nc):
        self.nc = nc
        self.ops = []
        self.pages = {}
        self.rdedupe = {}

    @staticmethod
    def region(ap):
        sp = str(ap.space)
        dsz = 2 if ap.dtype == BF else 4
        pairs = ap.ap
        off = int(ap.offset)
        if "DRAM" in sp:
            ext = sum((c - 1) * abs(s) for s, c in pairs) + 1
            return (ap.tensor.name, 0, 1, off * dsz, (off + ext) * dsz)
        rowb = 16384 if "PSUM" in sp else ARENA_BYTES
        rowel = rowb // dsz
        p0 = off // rowel
        f0 = off % rowel
        pc = pairs[0][1]
        ext = sum((c - 1) * abs(s) for s, c in pairs[1:]) + 1
        return (sp, p0, p0 + pc, f0 * dsz, (f0 + ext) * dsz)

    def _pages(self, r):
        pg = PAGE if r[0] in ("SB", "PSUM") else (1 << 20)
        return range(r[3] // pg, (r[4] - 1) // pg + 1)

    def add(self, eng, fn, reads, writes, dma=False):
        idx = len(self.ops)
        deps = set()
        rregs = [a if isinstance(a, tuple) else self.region(a) for a in reads]
        wregs = [a if isinstance(a, tuple) else self.region(a) for a in writes]
        for r in rregs:
            for pg in self._pages(r):
                for rec in self.pages.get((r[0], pg), ()):
                    if rec[6] and rec[7] and rec[1] < r[2] and r[1] < rec[2] and rec[3] < r[4] and r[3] < rec[4]:
                        deps.add(rec[5])
        for r in wregs:
            for pg in self._pages(r):
                lst = self.pages.get((r[0], pg))
                if not lst:
                    continue
                keep = []
                for rec in lst:
                    if not rec[7]:
                        continue
                    if rec[1] < r[2] and r[1] < rec[2] and rec[3] < r[4] and r[3] < rec[4]:
                        deps.add(rec[5])
                        if r[1] <= rec[1] and rec[2] <= r[2] and r[3] <= rec[3] and rec[4] <= r[4]:
                            rec[7] = False
                            continue
                    keep.append(rec)
                self.pages[(r[0], pg)] = keep
        for r in rregs:
            key = (eng, r)
            old = self.rdedupe.get(key)
            if old is not None and old[7] and eng != "sp" and not dma:
                old[5] = idx
                continue
            rec = [r[0], r[1], r[2], r[3], r[4], idx, False, True]
            self.rdedupe[key] = rec
            for pg in self._pages(r):
                self.pages.setdefault((r[0], pg), []).append(rec)
        for r in wregs:
            rec = [r[0], r[1], r[2], r[3], r[4], idx, True, True]
            for pg in self._pages(r):
                self.pages.setdefault((r[0], pg), []).append(rec)
        deps.discard(idx)
        self.ops.append([eng, fn, deps, False, dma])
        return idx

    def emit(self):
        nc = self.nc
        ops = self.ops
        engs = ["pe", "act", "dve", "pool", "sp"]
        NQ = {"sp": NDMA, "pool": NPDMA}
        isdma = [(o[0] == "sp") or (len(o) > 4 and o[4]) for o in ops]
        need = []
        for i, o in enumerate(ops):
            eng, deps = o[0], o[2]
            best = {}
            dmas = []
            for d in deps:
                de = ops[d][0]
                if isdma[d]:
                    dmas.append(d)
                else:
                    if de == "pe" and eng == "pe":
                        continue
                    if d > best.get(de, -1):
                        best[de] = d
            for d in best.values():
                ops[d][3] = True
            need.append((best, sorted(dmas)))
        cnt = {}
        run = {e: 0 for e in engs}
        dma_slot = {}
        dma_val = {}
        slot_run = {q: [0] * NQ[q] for q in NQ}
        ndma = {q: 0 for q in NQ}
        for i, o in enumerate(ops):
            eng, sig = o[0], o[3]
            if isdma[i]:
                s = ndma[eng] % NQ[eng]
                slot_run[eng][s] += 16
                dma_slot[i] = (eng, s)
                dma_val[i] = slot_run[eng][s]
                ndma[eng] += 1
            elif sig:
                run[eng] += 1
                cnt[i] = run[eng]
        import contextlib
        with contextlib.ExitStack() as es:
            sems = {e: es.enter_context(nc.semaphore("s_" + e)) for e in ["pe", "act", "dve", "pool"]}
            dsems = {(q, k): es.enter_context(nc.semaphore("s_%sdma%d" % (q, k))) for q in NQ for k in range(NQ[q])}
            block = es.enter_context(nc.Block())

            def stream(eng):
                def f(e):
                    seen = {x: 0 for x in ["pe", "act", "dve", "pool"]}
                    seen_d = {k: 0 for k in dsems}
                    for i, o in enumerate(ops):
                        en, fn, sig = o[0], o[1], o[3]
                        if en != eng:
                            continue
                        best, dmas = need[i]
                        for de, d in best.items():
                            c = cnt[d]
                            if c > seen[de]:
                                e.wait_ge(sems[de], c)
                                seen[de] = c
                        for d in dmas:
                            s, v = dma_slot[d], dma_val[d]
                            if v > seen_d[s]:
                                e.wait_ge(dsems[s], v)
                                seen_d[s] = v
                        if isdma[i]:
                            s = dma_slot[i]
                            prev = dma_val[i] - 16
                            if prev > seen_d[s]:
                                e.wait_ge(dsems[s], prev)
                                seen_d[s] = prev
                            fn(e).then_inc(dsems[s], 16)
                        else:
                            ins = fn(e)
                            if sig:
                                ins.then_inc(sems[en], 1)
                    if eng in NQ:
                        for k in range(NQ[eng]):
                            if slot_run[eng][k] > seen_d[(eng, k)]:
                                e.wait_ge(dsems[(eng, k)], slot_run[eng][k])
                return f

            block.tensor(stream("pe"))
            block.scalar(stream("act"))
            block.vector(stream("dve"))
            block.gpsimd(stream("pool"))
            block.sync(stream("sp"))


ARENA_BYTES = 206 * 1024


class Builder:
    def __init__(self, n_layers=DEPTH, n_tiles=8, TS=4, final=True, x_from_scratch=False):
        self.L = n_layers
        self.NT = n_tiles
        self.TS = TS
        self.TT = TS * 128
        self.final = final
        nc = bass.Bass("TRN2", target_bir_lowering=False)
        self.nc = nc
        self.S = Sched(nc)
        T = n_tiles * self.TT
        self.T = T
        self.x_in = nc.dram_tensor("x", [T, D], F32, kind="ExternalInput").ap()
        self.y_out = nc.dram_tensor("y", [T, D], F32, kind="ExternalOutput").ap()
        self.wsrc = nc.dram_tensor("wsrc", [n_layers, 128, NWL], F32, kind="ExternalInput").ap()
        self.cst = nc.dram_tensor("cst", [128, 3, 128], F32, kind="ExternalInput").ap()
        self.NPF = 8 + 8 + 2 + 96 + 24 + 16
        self.pf = nc.dram_tensor("pf", [n_layers, 128, self.NPF], F32, kind="ExternalInput").ap()
        self.NPR = 512 + 32 + 32 + 32
        self.pr = nc.dram_tensor("pr", [n_layers, 1, self.NPR], F32, kind="ExternalInput").ap()
        self.w2 = nc.dram_tensor("w2", [n_layers, 16, 512], F32, kind="ExternalInput").ap()
        self.fnw = nc.dram_tensor("fnw", [1, D], F32, kind="ExternalInput").ap()
        self.wbf = nc.dram_tensor("wbf", [n_layers, 128, NWL], BF, kind="Internal").ap()
        self.xscr = nc.dram_tensor("xscr", [T, D], F32, kind="Internal").ap()
        self.arena = nc.alloc_sbuf_tensor("arena", [128, ARENA_BYTES // 2], BF)
        self.pst = nc.alloc_psum_tensor("ps", [128, 4096], F32)
        self.top = 0
        self._rotc = {}
        self.rr = 0
        self.wq = []
        self.wnext = 0
        self.wissued = 0

    def alloc(self, nel, dt, parts=128, shape=None):
        dsz = 2 if dt == BF else 4
        nb = (nel * dsz + 63) // 64 * 64
        off = self.top
        self.top += nb
        assert self.top <= ARENA_BYTES, ("arena overflow", self.top)
        v = self.arena[0:parts, off // 2: off // 2 + nel * dsz // 2]
        if dt == F32:
            v = v.bitcast(F32)
        return v

    def rot(self, name):
        lst = getattr(self, name + "_l")
        k = self._rotc.get(name, 0)
        self._rotc[name] = k + 1
        v = lst[k % len(lst)]
        setattr(self, name, v)
        return v

    def alloc2(self, name, nel, dt, n=2):
        setattr(self, name + "_l", [self.alloc(nel, dt) for _ in range(n)])
        setattr(self, name, getattr(self, name + "_l")[0])

    def bank(self, b, dt=F32):
        v = self.pst[:, b * 512:(b + 1) * 512]
        if dt == BF:
            v = v.bitcast(BF)
        return v

    def nb(self):
        b = self.rr % 4
        self.rr += 1
        return b

    def mm(self, out, lhsT, rhs, start=True, stop=True):
        self.S.add("pe", lambda e: e.matmul(out, lhsT, rhs, start=start, stop=stop), [lhsT, rhs], [out])

    def tr(self, out, in_, ident):
        self.S.add("pe", lambda e: e.transpose(out, in_, ident), [in_, ident], [out])

    def act(self, out, in_, func, bias=None, scale=None, accum=None):
        kw = {}
        rd = [in_]
        wr = [out]
        if bias is not None:
            kw["bias"] = bias
            if not isinstance(bias, float):
                rd.append(bias)
        if scale is not None:
            kw["scale"] = scale
            if not isinstance(scale, float):
                rd.append(scale)
        if accum is not None:
            kw["accum_out"] = accum
            wr.append(accum)
        self.S.add("act", lambda e: e.activation(out, in_, func, **kw), rd, wr)

    def tt(self, eng, out, in0, in1, op):
        self.S.add(eng, lambda e: e.tensor_tensor(out, in0, in1, op), [in0, in1], [out])

    def ts(self, eng, out, in0, s1, s2, op0, op1=None, accum=None):
        rd = [in0] + [s for s in (s1, s2) if s is not None and not isinstance(s, float)]
        wr = [out] + ([accum] if accum is not None else [])
        kw = {}
        if op1 is not None:
            kw["op1"] = op1
        if accum is not None:
            kw["accum_out"] = accum
        self.S.add(eng, lambda e: e.tensor_scalar(out, in0, s1, s2, op0, **kw), rd, wr)

    def stt(self, eng, out, in0, sc, in1, op0, op1):
        rd = [in0, in1] + ([] if isinstance(sc, float) else [sc])
        self.S.add(eng, lambda e: e.scalar_tensor_tensor(out, in0, sc, in1, op0, op1), rd, [out])

    def rstd(self, out, ss, n):
        eps = self.EPSC[0:out.shape[0], 0:1]
        self.S.add("act", lambda e: e.activation(out, ss, AF.Ln, bias=eps, scale=1.0 / n), [ss, eps], [out])
        self.S.add("act", lambda e: e.activation(out, out, AF.Exp, scale=-0.5), [out], [out])

    def cp(self, eng, out, in_):
        if eng == "act":
            self.S.add("act", lambda e: e.activation(out, in_, AF.Copy), [in_], [out])
        else:
            self.S.add(eng, lambda e: e.tensor_copy(out, in_), [in_], [out])

    def ms(self, eng, out, val):
        self.S.add(eng, lambda e: e.memset(out, val), [], [out])

    def dma(self, out, in_, rreg=None, wreg=None, eng="sp"):
        rd = [] if in_.tensor.name in ("x", "wsrc", "cst", "pf", "pr", "w2", "fnw") else [in_]
        if rreg is not None:
            rd = [rreg]
        wr = [out] if wreg is None else [wreg]
        self.S.add(eng, lambda e: e.dma_start(out=out, in_=in_), rd, wr, dma=True)

    @staticmethod
    def v3(ap, a, b):
        return ap.rearrange("p (a b) -> p a b", a=a, b=b)

    @staticmethod
    def bcast(ap, pairs):
        return bass.AP(ap.tensor, ap.offset, [list(ap.ap[0])] + [list(p) for p in pairs])

    def plan_weights(self):
        q = []
        for l in range(self.L):
            for t in range(self.NT):
                for name, kind, a in STREAM:
                    o, n = STREAM_OFF[name]
                    q.append((l, o, n, name))
        self.wq = q

    def _issue_w(self):
        i = self.wissued
        l, o, n, name = self.wq[i]
        buf = self.wbuf[i % NWB]
        self.dma(buf[:, 0:n], self.wbf[l, :, o:o + n], rreg=("wbf", l, l + 1, o * 2, (o + n) * 2))
        self.wissued += 1

    def next_w(self, expect):
        i = self.wnext
        l, o, n, name = self.wq[i]
        assert name == expect, (name, expect)
        while self.wissued < min(len(self.wq), i + NWB):
            self._issue_w()
        self.wnext += 1
        return self.wbuf[i % NWB][:, 0:n]

    def build(self):
        nc = self.nc
        TS, TT = self.TS, self.TT
        v3 = self.v3
        cst = self.alloc(3 * 128, F32)
        c3 = v3(cst, 3, 128)
        IDf, ULE, UGT = c3[:, 0, :], c3[:, 1, :], c3[:, 2, :]
        self.dma(cst, self.cst.rearrange("p a b -> p (a b)"))
        IDb = self.alloc(128, BF)
        self.cp("dve", IDb, IDf)
        ONESf = self.alloc(128, F32)
        self.ms("pool", ONESf, 1.0)
        ONESb = self.alloc(128, BF)
        self.ms("pool", ONESb, 1.0)
        self.IDb, self.ULE, self.UGT, self.ONESf, self.ONESb = IDb, ULE, UGT, ONESf, ONESb
        self.EPSC = self.alloc(16, F32)
        self.ms("pool", self.EPSC, EPS)
        PF = self.alloc(self.NPF, F32)
        self.n1T, self.n2T, self.gnT = PF[:, 0:8], PF[:, 8:16], PF[:, 16:18]
        self.cwT = v3(PF[:, 18:114], 24, 4)
        self.cbT, self.snT = PF[:, 114:138], PF[:, 138:154]
        self.PF = PF
        PRf = self.alloc(self.NPR, F32, parts=1)
        PRb = self.alloc(self.NPR, BF, parts=1)
        self.PRf, self.PRb = PRf, PRb
        ABC = self.alloc(64, F32)
        self.ABC = ABC
        self.AROW = self.alloc(32, F32)
        W2f = self.alloc(512, F32, parts=16)
        self.W2f = W2f
        self.W2b = self.alloc(512, BF, parts=16)
        self.DI = self.alloc(32 * 128, BF)
        self.FNW = self.alloc(D, F32)
        self.dma(self.FNW, self.fnw.partition_broadcast(128).rearrange("p a b -> p (a b)"))
        self.Sg = self.alloc(1024, F32)
        self.Sgb = self.alloc(1024, BF)
        self.SS = self.alloc(2048, F32)
        self.HIST = self.alloc(24 * 3, F32)
        self.X = self.alloc(TS * D, F32)
        self.hT = self.alloc(8 * TT, BF)
        self.JUNK = self.alloc(1024, BF)
        self.st = self.alloc(64, F32)
        self.wbuf = [self.alloc(WMAX, BF) for _ in range(NWB)]
        self.YAG = self.alloc(8 * TT, BF)
        base = self.top
        self.convert_layer(0, 0, NCHUNK)
        self.top = base
        self.alloc_phases()
        self.plan_weights()
        for l in range(self.L):
            self.layer_setup(l)
            for t in range(self.NT):
                if l + 1 < self.L:
                    per = (NCHUNK + self.NT - 1) // self.NT
                    self.convert_layer(l + 1, t * per, min(NCHUNK, (t + 1) * per))
                self.tile(l, t)
        self.S.emit()
        return nc

    def alloc_phases(self):
        TS, TT = self.TS, self.TT
        base = self.top
        self.GLRT = self.alloc(TT, BF, parts=16)
        self.alloc2("LG", 512, F32)
        self.alloc2("E1", 512, F32)
        self.EBT = self.alloc(4 * TT, F32)
        self.ENBT = self.alloc(4 * TT, F32)
        self.ED = self.alloc(TS * 512, F32)
        self.QGT = self.alloc(4 * TT, BF)
        self.KGT = self.alloc(4 * TT, BF)
        self.KD = self.alloc(TS * 512, BF)
        self.V = self.alloc(TS * 1024, BF)
        self.SRW = self.alloc(8 * TT, BF)
        self.OGT = self.alloc(8 * TT, BF)
        self.alloc2("ATM", 512, BF)
        self.alloc2("ON", 1024, BF)
        self.alloc2("SGT", TT, F32)
        self.alloc2("OF", 1024, F32)
        gla_top = self.top
        self.top = base
        self.DT = self.alloc(TS * 32, F32)
        self.AA = self.alloc(TS * 32, F32)
        self.ECUM = self.alloc(TS * 32, F32)
        self.DEC = self.alloc(TS * 32, F32)
        self.ECL = self.alloc(TS * 32, F32)
        self.BT = self.alloc(4 * TT, BF)
        self.CT = self.alloc(4 * TT, BF)
        self.BTOK = self.alloc(TS * 512, BF)
        self.CBM = self.alloc(TS * 512, BF)
        off_xr = self.top
        self.XR = self.alloc(4 * (TT + 4), F32)
        self.ACC = self.alloc(4 * TT, F32)
        self.XC = self.alloc(4 * TT, BF)
        self.XS = self.alloc(TS * 512, BF)
        self.XDT = self.alloc(TS * 512, BF)
        self.SZ = self.alloc(TS * 512, BF)
        self.alloc2("AE", 512, F32)
        self.alloc2("EX", 512, F32, n=4)
        self.MTA = self.alloc(TS * 1024, BF)
        self.SSBV = self.alloc(TS * 512, BF)
        self.alloc2("YO", 512, F32)
        self.alloc2("YN", 512, BF)
        self.alloc2("XDD", 512, BF)
        off_ynt = self.top
        self.YNT = self.alloc(16 * TT, BF)
        self.XSTG = self.arena[:, off_ynt // 2: off_ynt // 2 + TS * D * 2].bitcast(F32)
        self.SGB = self.arena[:, off_xr // 2: off_xr // 2 + 8 * TT]
        ssd_top = self.top
        self.top = base
        self.ACTT = self.alloc(22 * TT, BF)
        self.SG = [self.alloc(TT, F32) for _ in range(2)]
        ffn_top = self.top
        self.top = max(gla_top, ssd_top, ffn_top)
        self.phase_tops = (base, gla_top, ssd_top, ffn_top)

    def convert_layer(self, l, c_lo, c_hi):
        csz = (NWL + NCHUNK - 1) // NCHUNK
        for c in range(c_lo, c_hi):
            c0 = c * csz
            n = min(csz, NWL - c0)
            if n <= 0:
                continue
            self.dma(self.wbf[l, :, c0:c0 + n], self.wsrc[l, :, c0:c0 + n],
                     wreg=("wbf", l, l + 1, c0 * 2, (c0 + n) * 2), eng="pool")

    def layer_setup(self, l):
        v3 = self.v3
        self.dma(self.PF, self.pf[l])
        self.dma(self.PRf, self.pr[l])
        self.cp("pool", self.PRb, self.PRf)
        self.dma(self.W2f, self.w2[l])
        self.cp("pool", self.W2b, self.W2f)
        self.dma(self.ABC, self.pr[l, :, 544:608].partition_broadcast(128).rearrange("p a b -> p (a b)"))
        self.act(self.AROW, self.ABC[:, 0:32], AF.Exp)
        self.ts("dve", self.AROW, self.AROW, -1.0, None, ALU.mult)
        DI3 = v3(self.DI, 32, 128)
        idb = self.bcast(self.IDb, [[0, 32], [1, 128]])
        dsk = self.bcast(self.ABC[:, 32:64], [[1, 32], [0, 128]])
        self.tt("pool", DI3, idb, dsk, ALU.mult)
        self.ms("pool", self.Sg, 0.0)
        self.ms("pool", self.Sgb, 0.0)
        self.ms("pool", self.SS, 0.0)
        self.ms("pool", self.HIST, 0.0)

    def norm_to_hT(self, nT, src=None):
        TS, TT = self.TS, self.TT
        X3 = self.v3(self.X if src is None else src, TS, D)
        hT3 = self.v3(self.hT, 8, TT)
        ss = self.st[:, 0:TS]
        rs = self.st[:, 8:8 + TS]
        for s in range(TS):
            self.act(self.JUNK, X3[:, s, :], AF.Square, accum=ss[:, s:s + 1])
        self.rstd(rs, ss, D)
        for s in range(TS):
            self.rot("ON")
            self.act(self.ON, X3[:, s, :], AF.Copy, scale=rs[:, s:s + 1])
            b = 4 + (s % 2)
            tb = self.bank(b, BF)
            for c in range(8):
                self.tr(tb[:, c * 128:(c + 1) * 128], self.ON[:, c * 128:(c + 1) * 128], self.IDb)
            self.tt("dve", hT3[:, :, s * 128:(s + 1) * 128], self.v3(tb, 8, 128),
                    self.bcast(nT, [[1, 8], [0, 128]]), ALU.mult)

    def proj_fm(self, W3, c0, rhs3, nk, bank):
        out = self.bank(bank)[:, 0:self.TT]
        for k in range(nk):
            self.mm(out, W3[:, k, c0:c0 + 128], rhs3[:, k, :], start=(k == 0), stop=(k == nk - 1))
        return out

    def proj_tm(self, lhs3, s, W3, ncols, nk, bank):
        out = self.bank(bank)[:, 0:ncols]
        for k in range(nk):
            self.mm(out, lhs3[:, k, s * 128:(s + 1) * 128], W3[:, k, 0:ncols], start=(k == 0), stop=(k == nk - 1))
        return out

    def tile(self, l, t):
        TS, TT = self.TS, self.TT
        v3 = self.v3
        bc = self.bcast
        X3 = v3(self.X, TS, D)
        hT3 = v3(self.hT, 8, TT)
        XSTG3 = v3(self.XSTG, TS, D)
        if l == 0 and t == 0:
            self.dma(XSTG3, self.x_in[0:TT, :].rearrange("(s p) d -> p s d", p=128))
        self.cp("pool", self.X, self.XSTG)
        import os
        stage = int(os.environ.get("KSTAGE", "99"))
        if stage <= 0:
            self.dma(self.y_out[t * TT:(t + 1) * TT, :].rearrange("(s p) d -> p s d", p=128), X3)
            return
        self.norm_to_hT(self.n1T, src=self.XSTG)
        def stop(n):
            if stage <= n:
                self.dma(self.y_out[t * TT:(t + 1) * TT, :].rearrange("(s p) d -> p s d", p=128), X3)
                return True
            return False
        if stop(1):
            return

        W = v3(self.next_w("SM"), 8, 48)
        b = self.nb()
        o = self.bank(b)[0:16, 0:TT]
        for k in range(8):
            self.mm(o, W[:, k, 0:16], hT3[:, k, :], start=(k == 0), stop=(k == 7))
        self.cp("act", self.GLRT, o)
        DT3 = v3(self.DTp, TS, 32)
        b = self.nb()
        dtb = self.bank(b)
        for s in range(TS):
            o = dtb[:, s * 32:(s + 1) * 32]
            for k in range(8):
                self.mm(o, hT3[:, k, s * 128:(s + 1) * 128], W[:, k, 16:48], start=(k == 0), stop=False)
            self.mm(o, self.ONESb[0:1, 0:128], self.PRb[0:1, 512:544], start=False, stop=True)
        self.act(self.DTe, dtb[:, 0:TS * 32], AF.Exp)
        self.act(self.DTp, self.DTe, AF.Ln, bias=1.0)

        if stop(2):
            return
        EBT3, ENBT3, ED3 = v3(self.EBT, 4, TT), v3(self.ENBT, 4, TT), v3(self.ED, TS, 512)
        for s in range(TS):
            sc = slice(s * 128, (s + 1) * 128)
            self.rot("E1")
            self.rot("LG")
            b = self.nb()
            lg = self.bank(b)
            self.mm(lg, self.GLRT[0:16, sc], self.W2b[0:16, :], start=True, stop=False)
            self.mm(lg, self.ONESb[0:1, 0:128], self.PRb[0:1, 0:512], start=False, stop=True)
            self.act(self.E1, lg, AF.Exp, scale=-1.0)
            self.act(self.LG, self.E1, AF.Ln, bias=1.0)
            b = self.nb()
            bt = self.bank(b)
            for h in range(4):
                self.mm(bt[:, h * 128:(h + 1) * 128], self.LG[:, h * 128:(h + 1) * 128], self.ULE)
            self.act(EBT3[:, :, sc], v3(bt, 4, 128), AF.Exp, scale=-1.0 / 16)
            self.act(ENBT3[:, :, sc], v3(bt, 4, 128), AF.Exp, scale=1.0 / 16)
            b = self.nb()
            dr = self.bank(b)
            self.mm(dr, self.UGT, self.LG)
            self.act(ED3[:, s, :], dr, AF.Exp, scale=-1.0 / 16)

        if stop(3):
            return
        QGT3, KGT3, KD3, V3_ = v3(self.QGT, 4, TT), v3(self.KGT, 4, TT), v3(self.KD, TS, 512), v3(self.V, TS, 1024)
        W = v3(self.next_w("Q"), 8, 512)
        for h in range(4):
            o = self.proj_fm(W, h * 128, hT3, 8, self.nb())
            self.stt("dve", QGT3[:, h, :], o, 128.0 ** -0.5, EBT3[:, h, :], ALU.mult, ALU.mult)
        W = v3(self.next_w("K"), 8, 512)
        for h in range(4):
            o = self.proj_fm(W, h * 128, hT3, 8, self.nb())
            self.tt("dve", KGT3[:, h, :], o, ENBT3[:, h, :], ALU.mult)
        for s in range(TS):
            o = self.proj_tm(hT3, s, W, 512, 8, self.nb())
            self.tt("dve", KD3[:, s, :], o, ED3[:, s, :], ALU.mult)
        for cb in range(2):
            W = v3(self.next_w("V%d" % cb), 8, 512)
            for s in range(TS):
                o = self.proj_tm(hT3, s, W, 512, 8, self.nb())
                self.cp("act", V3_[:, s, cb * 512:(cb + 1) * 512], o)
        SRW3 = v3(self.SRW, 8, TT)
        for cb in range(2):
            W = v3(self.next_w("R%d" % cb), 8, 512)
            for fb in range(4):
                j = cb * 4 + fb
                o = self.proj_fm(W, fb * 128, hT3, 8, self.nb())
                self.rot("SGT")
                self.act(self.SGT, o, AF.Silu)
                self.ts("dve", SRW3[:, j, :], self.SGT, self.gnT[:, j % 2:j % 2 + 1], None, ALU.mult)

        if stop(4):
            return
        S3, Sb3 = v3(self.Sg, 4, 256), v3(self.Sgb, 4, 256)
        OGT3 = v3(self.OGT, 8, TT)
        for s in range(TS):
            sc = slice(s * 128, (s + 1) * 128)
            ATM3 = v3(self.rot("ATM"), 4, 128)
            self.rot("OF")
            self.rot("ON")
            oss = self.st[:, 16 + 8 * (s % 2):20 + 8 * (s % 2)]
            ors = self.st[:, 20 + 8 * (s % 2):24 + 8 * (s % 2)]
            b = self.nb()
            at = self.bank(b)
            for h in range(4):
                self.mm(at[:, h * 128:(h + 1) * 128], KGT3[:, h, sc], QGT3[:, h, sc])
            self.tt("dve", ATM3, v3(at, 4, 128), bc(self.ULE, [[0, 4], [1, 128]]), ALU.mult)
            sub = int(os.environ.get("KSUB", "99"))
            if sub <= 0:
                continue
            ob = [self.bank(4), self.bank(5)]
            db = [self.bank(6), self.bank(7)]
            for h in range(4):
                o = ob[h // 2][:, (h % 2) * 256:(h % 2 + 1) * 256]
                self.mm(o, ATM3[:, h, :], V3_[:, s, h * 256:(h + 1) * 256], start=True, stop=False)
                self.mm(o, QGT3[:, h, sc], Sb3[:, h, :], start=False, stop=True)
            if sub <= 1:
                continue
            for h in range(4):
                d = db[h // 2][:, (h % 2) * 256:(h % 2 + 1) * 256]
                self.mm(d, KD3[:, s, h * 128:(h + 1) * 128], V3_[:, s, h * 256:(h + 1) * 256])
            if sub <= 2:
                continue
            for h in range(4):
                d = db[h // 2][:, (h % 2) * 256:(h % 2 + 1) * 256]
                self.stt("dve", S3[:, h, :], S3[:, h, :], EBT3[:, h, s * 128 + 127:s * 128 + 128], d, ALU.mult, ALU.add)
                self.cp("act", Sb3[:, h, :], S3[:, h, :])
            if sub <= 3:
                continue
            for h in range(4):
                o = ob[h // 2][:, (h % 2) * 256:(h % 2 + 1) * 256]
                self.cp("dve", self.OF[:, h * 256:(h + 1) * 256], o)
                self.act(self.JUNK[:, 0:256], self.OF[:, h * 256:(h + 1) * 256], AF.Square, accum=oss[:, h:h + 1])
            sub2 = int(os.environ.get("KSUB2", "99"))
            if sub2 <= 0:
                continue
            self.rstd(ors, oss, 256)
            if sub2 <= 1:
                continue
            for h in range(4):
                o = ob[h // 2][:, (h % 2) * 256:(h % 2 + 1) * 256]
                self.act(self.ON[:, h * 256:(h + 1) * 256], self.OF[:, h * 256:(h + 1) * 256], AF.Copy, scale=ors[:, h:h + 1])
            if sub <= 4:
                continue
            b = self.nb()
            tb = self.bank(b, BF)
            for j in range(8):
                self.tr(tb[:, j * 128:(j + 1) * 128], self.ON[:, j * 128:(j + 1) * 128], self.IDb)
            self.tt("dve", OGT3[:, :, sc], v3(tb, 8, 128), SRW3[:, :, sc], ALU.mult)

        if stop(5):
            return
        YAG3 = v3(self.YAG, 8, TT)
        for cb in range(2):
            W = v3(self.next_w("GA%d" % cb), 8, 512)
            for fb in range(4):
                o = self.proj_fm(W, fb * 128, hT3, 8, self.nb())
                self.act(YAG3[:, cb * 4 + fb, :], o, AF.Sigmoid)
        for cb in range(2):
            W = v3(self.next_w("WYA%d" % cb), 8, 512)
            for fb in range(4):
                o = self.proj_fm(W, fb * 128, OGT3, 8, self.nb())
                self.tt("dve", YAG3[:, cb * 4 + fb, :], o, YAG3[:, cb * 4 + fb, :], ALU.mult)

        if stop(6):
            return
        self.ssd(l, t)
        if stop(7):
            return

        SGB3 = v3(self.SGB, 8, TT)
        YNT3 = v3(self.YNT, 16, TT)
        for cb in range(2):
            W = v3(self.next_w("GB%d" % cb), 8, 512)
            for fb in range(4):
                o = self.proj_fm(W, fb * 128, hT3, 8, self.nb())
                self.act(SGB3[:, cb * 4 + fb, :], o, AF.Sigmoid)
        for cb in range(4):
            W = v3(self.next_w("WYB%d" % cb), 16, 256)
            for fb in range(2):
                j = cb * 2 + fb
                o = self.proj_fm(W, fb * 128, YNT3, 16, self.nb())
                self.tt("dve", SGB3[:, j, :], o, SGB3[:, j, :], ALU.mult)
                self.tt("dve", SGB3[:, j, :], SGB3[:, j, :], YAG3[:, j, :], ALU.add)
        for cb in range(2):
            W = v3(self.next_w("WO%d" % cb), 8, 512)
            for s in range(TS):
                o = self.proj_tm(SGB3, s, W, 512, 8, self.nb())
                xs_ = X3[:, s, cb * 512:(cb + 1) * 512]
                self.tt("dve", xs_, xs_, o, ALU.add)

        if stop(8):
            return
        self.norm_to_hT(self.n2T)
        nl, nt = (l, t + 1) if t + 1 < self.NT else (l + 1, 0)
        if nl < self.L:
            nsrc = self.x_in if nl == 0 else self.xscr
            self.dma(XSTG3, nsrc[nt * TT:(nt + 1) * TT, :].rearrange("(s p) d -> p s d", p=128))
        ACTT3 = v3(self.ACTT, 22, TT)
        for i in range(11):
            W = v3(self.next_w("FI%d" % i), 8, 512)
            for q in range(2):
                og = self.proj_fm(W, q * 128, hT3, 8, self.nb())
                ou = self.proj_fm(W, 256 + q * 128, hT3, 8, self.nb())
                sg = self.SG[q]
                self.act(sg, og, AF.Silu)
                self.tt("dve", ACTT3[:, i * 2 + q, :], sg, ou, ALU.mult)
        for cb in range(4):
            W = v3(self.next_w("FO%d" % cb), 22, 256)
            for s in range(TS):
                o = self.proj_tm(ACTT3, s, W, 256, 22, self.nb())
                xs_ = X3[:, s, cb * 256:(cb + 1) * 256]
                self.tt("dve", xs_, xs_, o, ALU.add)

        rows = slice(t * TT, (t + 1) * TT)
        if l == self.L - 1 and self.final:
            ss = self.st[:, 0:TS]
            rs = self.st[:, 8:8 + TS]
            for s in range(TS):
                self.act(self.JUNK, X3[:, s, :], AF.Square, accum=ss[:, s:s + 1])
            self.rstd(rs, ss, D)
            for s in range(TS):
                self.stt("dve", X3[:, s, :], X3[:, s, :], rs[:, s:s + 1], self.FNW, ALU.mult, ALU.mult)
            self.dma(self.y_out[rows, :].rearrange("(s p) d -> p s d", p=128), X3)
        elif l == self.L - 1:
            self.dma(self.y_out[rows, :].rearrange("(s p) d -> p s d", p=128), X3)
        else:
            self.dma(self.xscr[rows, :].rearrange("(s p) d -> p s d", p=128), X3)

    def ssd(self, l, t):
        TS, TT = self.TS, self.TT
        v3 = self.v3
        bc = self.bcast
        hT3 = v3(self.hT, 8, TT)
        DT3, AA3 = v3(self.DT, TS, 32), v3(self.AA, TS, 32)
        self.cp("pool", self.DT, self.DTp)
        self.tt("dve", AA3, DT3, bc(self.AROW, [[0, TS], [1, 32]]), ALU.mult)
        b = self.nb()
        cb_ = self.bank(b)
        for s in range(TS):
            a_s = self.AA[:, s * 32:(s + 1) * 32]
            self.mm(cb_[:, s * 32:(s + 1) * 32], self.ULE, a_s)
            self.mm(cb_[:, 128 + s * 32:128 + (s + 1) * 32], self.UGT, a_s)
            self.mm(cb_[:, 256 + s * 32:256 + (s + 1) * 32], self.ONESf, a_s)
        self.act(self.ECUM, cb_[:, 0:TS * 32], AF.Exp)
        self.act(self.DEC, cb_[:, 128:128 + TS * 32], AF.Exp)
        self.act(self.ECL, cb_[:, 256:256 + TS * 32], AF.Exp)
        ECUM3, DEC3, ECL3 = v3(self.ECUM, TS, 32), v3(self.DEC, TS, 32), v3(self.ECL, TS, 32)

        import os
        kssd = int(os.environ.get("KSSD", "99"))
        if kssd <= 0:
            return
        XR3 = v3(self.XR, 4, TT + 4)
        ACC3 = v3(self.ACC, 4, TT)
        HIST3 = v3(self.HIST, 24, 3)

        def conv_proj(W, fb0):
            self.cp("pool", XR3[:, :, 0:3], HIST3[:, fb0:fb0 + 4, :])
            for fb in range(4):
                o = self.proj_fm(W, fb * 128, hT3, 8, self.nb())
                self.cp("act", XR3[:, fb, 3:3 + TT], o)
            self.cp("pool", HIST3[:, fb0:fb0 + 4, :], XR3[:, :, TT:TT + 3])

        def conv_apply(fb0, out3):
            for fb in range(4):
                cw = self.cwT[:, fb0 + fb, :]
                self.ts("dve", ACC3[:, fb, :], XR3[:, fb, 0:TT], cw[:, 0:1], None, ALU.mult)
                for k in range(1, 4):
                    self.stt("dve", ACC3[:, fb, :], XR3[:, fb, k:k + TT], cw[:, k:k + 1], ACC3[:, fb, :], ALU.mult, ALU.add)
                self.act(out3[:, fb, :], ACC3[:, fb, :], AF.Silu, bias=self.cbT[:, fb0 + fb:fb0 + fb + 1])

        def conv_block(W, fb0, out3):
            conv_proj(W, fb0)
            conv_apply(fb0, out3)

        BT3, CT3 = v3(self.BT, 4, TT), v3(self.CT, 4, TT)
        conv_block(v3(self.next_w("B"), 8, 512), 16, BT3)
        conv_block(v3(self.next_w("C"), 8, 512), 20, CT3)
        if kssd <= 1:
            return
        BTOK3 = v3(self.BTOK, TS, 512)
        CBM = self.CBM
        for s in range(TS):
            sc = slice(s * 128, (s + 1) * 128)
            b = self.nb()
            tb = self.bank(b, BF)
            for g in range(4):
                self.tr(tb[:, g * 128:(g + 1) * 128], BT3[:, g, sc], self.IDb)
            self.cp("act", BTOK3[:, s, :], tb[:, 0:512])
            b = self.nb()
            cbk = self.bank(b)
            for g in range(4):
                self.mm(cbk[:, g * 128:(g + 1) * 128], BT3[:, g, sc], CT3[:, g, sc])
            self.tt("dve", v3(CBM[:, s * 512:(s + 1) * 512], 4, 128), v3(cbk, 4, 128),
                    bc(self.ULE, [[0, 4], [1, 128]]), ALU.mult)

        if kssd <= 2:
            return
        XC3 = v3(self.XC, 4, TT)
        XS3, XDT3, SZ3 = v3(self.XS, TS, 512), v3(self.XDT, TS, 512), v3(self.SZ, TS, 512)
        YNT3 = v3(self.YNT, 16, TT)
        DI3 = v3(self.DI, 32, 128)
        SS3 = v3(self.SS, 4, 512)
        yss = self.st[:, 24:25]
        yrs = self.st[:, 25:26]
        for g in range(4):
            conv_proj(v3(self.next_w("XS%d" % g), 8, 512), g * 4)
            W = v3(self.next_w("Z%d" % g), 8, 512)
            for s in range(TS):
                o = self.proj_tm(hT3, s, W, 512, 8, self.nb())
                self.act(SZ3[:, s, :], o, AF.Silu)
            conv_apply(g * 4, XC3)
            for s in range(TS):
                sc = slice(s * 128, (s + 1) * 128)
                tb = self.bank(4 + (s % 2), BF)
                for fb in range(4):
                    self.tr(tb[:, fb * 128:(fb + 1) * 128], XC3[:, fb, sc], self.IDb)
                self.cp("act", XS3[:, s, :], tb[:, 0:512])
                if os.environ.get("KNOXDT") != "1":
                    self.tt("dve", v3(XDT3[:, s, :], 8, 64), v3(XS3[:, s, :], 8, 64),
                            bc(DT3[:, s, g * 8:(g + 1) * 8], [[1, 8], [0, 64]]), ALU.mult)
            SSBV3 = v3(self.SSBV, TS, 512)
            MTA5 = self.MTA.rearrange("p (s h e i) -> p s h e i", s=TS, h=2, e=4, i=128)
            self.cp("act", SSBV3[:, 0, :], SS3[:, g, :])
            pend = []

            def flush_mt():
                for (ss_, half_, ex_) in pend:
                    cbm = CBM[:, ss_ * 512 + g * 128: ss_ * 512 + (g + 1) * 128]
                    self.tt("dve", MTA5[:, ss_, half_, :, :], v3(ex_, 4, 128), bc(cbm, [[0, 4], [1, 128]]), ALU.mult)
                del pend[:]

            for s in range(TS):
                self.rot("XDD")
                self.tt("dve", v3(self.XDD, 8, 64), v3(XDT3[:, s, :], 8, 64),
                        bc(DEC3[:, s, g * 8:(g + 1) * 8], [[1, 8], [0, 64]]), ALU.mult)
                ds = self.bank(self.nb())
                self.mm(ds, BTOK3[:, s, g * 128:(g + 1) * 128], self.XDD)
                newp = []
                for half in range(2):
                    e0 = g * 8 + half * 4
                    self.rot("AE")
                    self.rot("EX")
                    AE3 = v3(self.AE, 4, 128)
                    self.tt("dve", AE3, bc(self.UGT, [[0, 4], [1, 128]]),
                            bc(AA3[:, s, e0:e0 + 4], [[1, 4], [0, 128]]), ALU.mult)
                    sb = self.bank(self.nb())
                    for ei in range(4):
                        self.mm(sb[:, ei * 128:(ei + 1) * 128], AE3[:, ei, :], self.ULE)
                    self.act(self.EX, sb, AF.Exp)
                    newp.append((s, half, self.EX))
                flush_mt()
                pend.extend(newp)
                self.tt("dve", v3(SS3[:, g, :], 8, 64), v3(SS3[:, g, :], 8, 64),
                        bc(ECL3[:, s, g * 8:(g + 1) * 8], [[1, 8], [0, 64]]), ALU.mult)
                self.tt("dve", SS3[:, g, :], SS3[:, g, :], ds, ALU.add)
                if s < TS - 1:
                    self.cp("act", SSBV3[:, s + 1, :], SS3[:, g, :])
            flush_mt()

            st_b2 = []
            st_b3 = []

            def run_b3():
                for (ss_, yn_) in st_b3:
                    tb = self.bank(4 + (ss_ % 2), BF)
                    for fb in range(4):
                        self.tr(tb[:, fb * 128:(fb + 1) * 128], yn_[:, fb * 128:(fb + 1) * 128], self.IDb)
                    self.tt("dve", YNT3[:, g * 4:(g + 1) * 4, ss_ * 128:(ss_ + 1) * 128], v3(tb[:, 0:512], 4, 128),
                            bc(self.snT[:, g * 4:(g + 1) * 4], [[1, 4], [0, 128]]), ALU.mult)
                del st_b3[:]

            def run_b2():
                for (ss_, yo_, yn_, yss_, yrs_) in st_b2:
                    self.rstd(yrs_, yss_, 512)
                    self.act(yn_, yo_, AF.Copy, scale=yrs_)
                    st_b3.append((ss_, yn_))
                del st_b2[:]

            for s in range(TS):
                sc = slice(s * 128, (s + 1) * 128)
                self.rot("YO")
                self.rot("YN")
                k3 = s % 3
                yss = self.st[:, 40 + k3 * 2:41 + k3 * 2]
                yrs = self.st[:, 41 + k3 * 2:42 + k3 * 2]
                yo = self.bank(self.nb())
                self.mm(yo, CT3[:, g, sc], SSBV3[:, s, :])
                self.cp("act", self.YO, yo)
                self.tt("dve", v3(self.YO, 8, 64), v3(self.YO, 8, 64),
                        bc(ECUM3[:, s, g * 8:(g + 1) * 8], [[1, 8], [0, 64]]), ALU.mult)
                yb = self.bank(6 + (s % 2))
                for half in range(2):
                    e0 = g * 8 + half * 4
                    for ei in range(4):
                        c0 = (half * 4 + ei) * 64
                        self.mm(yb[:, c0:c0 + 64], MTA5[:, s, half, ei, :], XDT3[:, s, c0:c0 + 64], start=True, stop=False)
                        self.mm(yb[:, c0:c0 + 64], DI3[:, e0 + ei, :], XS3[:, s, c0:c0 + 64], start=False, stop=True)
                run_b3()
                self.tt("dve", self.YO, yb, self.YO, ALU.add)
                self.tt("dve", self.YO, self.YO, SZ3[:, s, :], ALU.mult)
                self.act(self.JUNK[:, 0:512], self.YO, AF.Square, accum=yss)
                run_b2()
                st_b2.append((s, self.YO, self.YN, yss, yrs))
            run_b3()
            run_b2()
            run_b3()


def build_program(n_layers=DEPTH, n_tiles=8, TS=4, final=True):
    B = Builder(n_layers, n_tiles, TS, final)
    B.DTp = B.alloc(TS * 32, F32)
    B.DTe = B.alloc(TS * 32, F32)
    B.build()
    return B.nc


def host_consts():
    c = np.zeros((128, 3, 128), np.float32)
    j = np.arange(128)[:, None]
    i = np.arange(128)[None, :]
    c[:, 0, :] = (j == i)
    c[:, 1, :] = (j <= i)
    c[:, 2, :] = (j > i)
    return c


def host_params(inp, n_layers=DEPTH):
    fm = lambda v, nb: np.ascontiguousarray(v.reshape(nb, 128).T)
    pf, pr, ws = [], [], []
    for l in range(n_layers):
        cw = inp["ssm_conv_w"][l]
        cwT = np.ascontiguousarray(cw.reshape(4, 24, 128).transpose(2, 1, 0)).reshape(128, 96)
        pf.append(np.concatenate([fm(inp["norm1_w"][l], 8), fm(inp["norm2_w"][l], 8), fm(inp["gla_norm_w"][l], 2),
                                  cwT, fm(inp["ssm_conv_b"][l], 24), fm(inp["ssm_norm_w"][l], 16)], axis=1))
        pr.append(np.concatenate([inp["gla_gate_b"][l], inp["ssm_dt_bias"][l], inp["ssm_A_log"][l],
                                  inp["ssm_D"][l]])[None, :])
        ws.append(host_weight_stream(inp["w_in"][l], inp["w_branch_a"][l], inp["w_branch_b"][l],
                                     inp["w_mix_out"][l], inp["w_ffn_in"][l], inp["w_ffn_out"][l]))
    return (np.ascontiguousarray(np.stack(pf)).astype(np.float32), np.ascontiguousarray(np.stack(pr)).astype(np.float32),
            np.stack(ws))


_CACHE = {}


def kernel(**inputs):
    inp = {k: np.asarray(v) for k, v in inputs.items()}
    x = inp["x"]
    pf, pr, ws = host_params(inp)
    if "nc" not in _CACHE:
        _CACHE["nc"] = build_program()
    nc = _CACHE["nc"]
    shared = dict(wsrc=ws, cst=host_consts(), pf=pf, pr=pr, w2=np.ascontiguousarray(inp["gla_gate_w2"]),
                  fnw=np.ascontiguousarray(inp["final_norm_w"][None, :]))
    in_maps = [dict(shared, x=np.ascontiguousarray(x[c])) for c in range(NCORES)]
    res = run_bass_kernel_spmd(nc, in_maps, core_ids=list(range(NCORES)))
    return np.stack([np.asarray(r["y"]) for r in res.results]).astype(np.float32)
```

```python
import numpy as np
import concourse.bass as bass
import concourse.mybir as mybir
from concourse.bass_utils import run_bass_kernel_spmd

F32 = mybir.dt.float32
BF = mybir.dt.bfloat16
ALU = mybir.AluOpType
AF = mybir.ActivationFunctionType

D = 1024
SEQ = 4096
DEPTH = 4
NCORES = 8
EPS = 1e-6
FFN = 2816
WCH = 4096
WMAX = 5632
NWB = 3
NCHUNK = 32
NDMA = 12
NPDMA = 6

W_IN_OFF = dict(q=0, k=512, v=1024, r=2048, glr=3072, z=3088, xbc=5136, dt=8208, ga=8240, gb=9264)


def _stream_spec():
    spec = [("SM", "sm", None), ("Q", "in", W_IN_OFF["q"]), ("K", "in", W_IN_OFF["k"])]
    spec += [("V%d" % i, "in", W_IN_OFF["v"] + 512 * i) for i in range(2)]
    spec += [("R%d" % i, "in", W_IN_OFF["r"] + 512 * i) for i in range(2)]
    spec += [("GA%d" % i, "in", W_IN_OFF["ga"] + 512 * i) for i in range(2)]
    spec += [("WYA%d" % i, "ya", 512 * i) for i in range(2)]
    spec += [("B", "in", W_IN_OFF["xbc"] + 2048), ("C", "in", W_IN_OFF["xbc"] + 2560)]
    for g in range(4):
        spec += [("XS%d" % g, "in", W_IN_OFF["xbc"] + 512 * g), ("Z%d" % g, "in", W_IN_OFF["z"] + 512 * g)]
    spec += [("GB%d" % i, "in", W_IN_OFF["gb"] + 512 * i) for i in range(2)]
    spec += [("WYB%d" % i, "yb", 256 * i) for i in range(4)]
    spec += [("WO%d" % i, "out", 512 * i) for i in range(2)]
    spec += [("FI%d" % i, "fi", i) for i in range(11)]
    spec += [("FO%d" % i, "fo", 256 * i) for i in range(4)]
    return spec


def _blk_elems(kind):
    return {"sm": 8 * 48, "in": 4096, "ya": 4096, "yb": 4096, "out": 4096, "fi": 4096, "fo": 22 * 256}[kind]


STREAM = _stream_spec()
STREAM_OFF = {}
_o = 0
for _n, _k, _a in STREAM:
    STREAM_OFF[_n] = (_o, _blk_elems(_k))
    _o += _blk_elems(_k)
NWL = _o


def _tile_k(mat, c0, nc_):
    K = mat.shape[0]
    return mat[:, c0:c0 + nc_].reshape(K // 128, 128, nc_).transpose(1, 0, 2).reshape(128, -1)


def host_weight_stream(w_in, w_ya, w_yb, w_out, w_fi, w_fo):
    out = np.empty((128, NWL), np.float32)
    for name, kind, a in STREAM:
        o, n = STREAM_OFF[name]
        if kind == "sm":
            m = np.concatenate([w_in[:, W_IN_OFF["glr"]:W_IN_OFF["glr"] + 16],
                                w_in[:, W_IN_OFF["dt"]:W_IN_OFF["dt"] + 32]], axis=1)
            blk = _tile_k(m, 0, 48)
        elif kind == "in":
            blk = _tile_k(w_in, a, 512)
        elif kind == "ya":
            blk = _tile_k(w_ya, a, 512)
        elif kind == "yb":
            blk = _tile_k(w_yb, a, 256)
        elif kind == "out":
            blk = _tile_k(w_out, a, 512)
        elif kind == "fi":
            m = np.concatenate([w_fi[:, a * 256:(a + 1) * 256], w_fi[:, FFN + a * 256:FFN + (a + 1) * 256]], axis=1)
            blk = _tile_k(m, 0, 512)
        elif kind == "fo":
            blk = _tile_k(w_fo, a, 256)
        out[:, o:o + n] = blk
    return out


PAGE = 2048


class Sched:
    def __init__(self, nc):
        self.nc = nc
        self.ops = []
        self.pages = {}
        self.rdedupe = {}

    @staticmethod
    def region(ap):
        sp = str(ap.space)
        dsz = 2 if ap.dtype == BF else 4
        pairs = ap.ap
        off = int(ap.offset)
        if "DRAM" in sp:
            ext = sum((c - 1) * abs(s) for s, c in pairs) + 1
            return (ap.tensor.name, 0, 1, off * dsz, (off + ext) * dsz)
        rowb = 16384 if "PSUM" in sp else ARENA_BYTES
        rowel = rowb // dsz
        p0 = off // rowel
        f0 = off % rowel
        pc = pairs[0][1]
        ext = sum((c - 1) * abs(s) for s, c in pairs[1:]) + 1
        return (sp, p0, p0 + pc, f0 * dsz, (f0 + ext) * dsz)

    def _pages(self, r):
        pg = PAGE if r[0] in ("SB", "PSUM") else (1 << 20)
        return range(r[3] // pg, (r[4] - 1) // pg + 1)

    def add(self, eng, fn, reads, writes, dma=False):
        idx = len(self.ops)
        deps = set()
        rregs = [a if isinstance(a, tuple) else self.region(a) for a in reads]
        wregs = [a if isinstance(a, tuple) else self.region(a) for a in writes]
        for r in rregs:
            for pg in self._pages(r):
                for rec in self.pages.get((r[0], pg), ()):
                    if rec[6] and rec[7] and rec[1] < r[2] and r[1] < rec[2] and rec[3] < r[4] and r[3] < rec[4]:
                        deps.add(rec[5])
        for r in wregs:
            for pg in self._pages(r):
                lst = self.pages.get((r[0], pg))
                if not lst:
                    continue
                keep = []
                for rec in lst:
                    if not rec[7]:
                        continue
                    if rec[1] < r[2] and r[1] < rec[2] and rec[3] < r[4] and r[3] < rec[4]:
                        deps.add(rec[5])
                        if r[1] <= rec[1] and rec[2] <= r[2] and r[3] <= rec[3] and rec[4] <= r[4]:
                            rec[7] = False
                            continue
                    keep.append(rec)
                self.pages[(r[0], pg)] = keep
        for r in rregs:
            key = (eng, r)
            old = self.rdedupe.get(key)
            if old is not None and old[7] and eng != "sp" and not dma:
                old[5] = idx
                continue
            rec = [r[0], r[1], r[2], r[3], r[4], idx, False, True]
            self.rdedupe[key] = rec
            for pg in self._pages(r):
                self.pages.setdefault((r[0], pg), []).append(rec)
        for r in wregs:
            rec = [r[0], r[1], r[2], r[3], r[4], idx, True, True]
            for pg in self._pages(r):
                self.pages.setdefault((r[0], pg), []).append(rec)
        deps.discard(idx)
        self.ops.append([eng, fn, deps, False, dma])
        return idx

    def emit(self):
        nc = self.nc
        ops = self.ops
        engs = ["pe", "act", "dve", "pool", "sp"]
        NQ = {"sp": NDMA, "pool": NPDMA}
        isdma = [(o[0] == "sp") or (len(o) > 4 and o[4]) for o in ops]
        need = []
        for i, o in enumerate(ops):
            eng, deps = o[0], o[2]
            best = {}
            dmas = []
            for d in deps:
                de = ops[d][0]
                if isdma[d]:
                    dmas.append(d)
                else:
                    if de == "pe" and eng == "pe":
                        continue
                    if d > best.get(de, -1):
                        best[de] = d
            for d in best.values():
                ops[d][3] = True
            need.append((best, sorted(dmas)))
        cnt = {}
        run = {e: 0 for e in engs}
        dma_slot = {}
        dma_val = {}
        slot_run = {q: [0] * NQ[q] for q in NQ}
        ndma = {q: 0 for q in NQ}
        for i, o in enumerate(ops):
            eng, sig = o[0], o[3]
            if isdma[i]:
                s = ndma[eng] % NQ[eng]
                slot_run[eng][s] += 16
                dma_slot[i] = (eng, s)
                dma_val[i] = slot_run[eng][s]
                ndma[eng] += 1
            elif sig:
                run[eng] += 1
                cnt[i] = run[eng]
        import contextlib
        with contextlib.ExitStack() as es:
            sems = {e: es.enter_context(nc.semaphore("s_" + e)) for e in ["pe", "act", "dve", "pool"]}
            dsems = {(q, k): es.enter_context(nc.semaphore("s_%sdma%d" % (q, k))) for q in NQ for k in range(NQ[q])}
            block = es.enter_context(nc.Block())

            def stream(eng):
                def f(e):
                    seen = {x: 0 for x in ["pe", "act", "dve", "pool"]}
                    seen_d = {k: 0 for k in dsems}
                    for i, o in enumerate(ops):
                        en, fn, sig = o[0], o[1], o[3]
                        if en != eng:
                            continue
                        best, dmas = need[i]
                        for de, d in best.items():
                            c = cnt[d]
                            if c > seen[de]:
                                e.wait_ge(sems[de], c)
                                seen[de] = c
                        for d in dmas:
                            s, v = dma_slot[d], dma_val[d]
                            if v > seen_d[s]:
                                e.wait_ge(dsems[s], v)
                                seen_d[s] = v
                        if isdma[i]:
                            s = dma_slot[i]
                            prev = dma_val[i] - 16
                            if prev > seen_d[s]:
                                e.wait_ge(dsems[s], prev)
                                seen_d[s] = prev
                            fn(e).then_inc(dsems[s], 16)
                        else:
                            ins = fn(e)
                            if sig:
                                ins.then_inc(sems[en], 1)
                    if eng in NQ:
                        for k in range(NQ[eng]):
                            if slot_run[eng][k] > seen_d[(eng, k)]:
                                e.wait_ge(dsems[(eng, k)], slot_run[eng][k])
                return f

            block.tensor(stream("pe"))
            block.scalar(stream("act"))
            block.vector(stream("dve"))
            block.gpsimd(stream("pool"))
            block.sync(stream("sp"))


ARENA_BYTES = 206 * 1024


class Builder:
    def __init__(self, n_layers=DEPTH, n_tiles=8, TS=4, final=True, x_from_scratch=False):
        self.L = n_layers
        self.NT = n_tiles
        self.TS = TS
        self.TT = TS * 128
        self.final = final
        nc = bass.Bass("TRN2", target_bir_lowering=False)
        self.nc = nc
        self.S = Sched(nc)
        T = n_tiles * self.TT
        self.T = T
        self.x_in = nc.dram_tensor("x", [T, D], F32, kind="ExternalInput").ap()
        self.y_out = nc.dram_tensor("y", [T, D], F32, kind="ExternalOutput").ap()
        self.wsrc = nc.dram_tensor("wsrc", [n_layers, 128, NWL], F32, kind="ExternalInput").ap()
        self.cst = nc.dram_tensor("cst", [128, 3, 128], F32, kind="ExternalInput").ap()
        self.NPF = 8 + 8 + 2 + 96 + 24 + 16
        self.pf = nc.dram_tensor("pf", [n_layers, 128, self.NPF], F32, kind="ExternalInput").ap()
        self.NPR = 512 + 32 + 32 + 32
        self.pr = nc.dram_tensor("pr", [n_layers, 1, self.NPR], F32, kind="ExternalInput").ap()
        self.w2 = nc.dram_tensor("w2", [n_layers, 16, 512], F32, kind="ExternalInput").ap()
        self.fnw = nc.dram_tensor("fnw", [1, D], F32, kind="ExternalInput").ap()
        self.wbf = nc.dram_tensor("wbf", [n_layers, 128, NWL], BF, kind="Internal").ap()
        self.xscr = nc.dram_tensor("xscr", [T, D], F32, kind="Internal").ap()
        self.arena = nc.alloc_sbuf_tensor("arena", [128, ARENA_BYTES // 2], BF)
        self.pst = nc.alloc_psum_tensor("ps", [128, 4096], F32)
        self.top = 0
        self._rotc = {}
        self.rr = 0
        self.wq = []
        self.wnext = 0
        self.wissued = 0

    def alloc(self, nel, dt, parts=128, shape=None):
        dsz = 2 if dt == BF else 4
        nb = (nel * dsz + 63) // 64 * 64
        off = self.top
        self.top += nb
        assert self.top <= ARENA_BYTES, ("arena overflow", self.top)
        v = self.arena[0:parts, off // 2: off // 2 + nel * dsz // 2]
        if dt == F32:
            v = v.bitcast(F32)
        return v

    def rot(self, name):
        lst = getattr(self, name + "_l")
        k = self._rotc.get(name, 0)
        self._rotc[name] = k + 1
        v = lst[k % len(lst)]
        setattr(self, name, v)
        return v

    def alloc2(self, name, nel, dt, n=2):
        setattr(self, name + "_l", [self.alloc(nel, dt) for _ in range(n)])
        setattr(self, name, getattr(self, name + "_l")[0])

    def bank(self, b, dt=F32):
        v = self.pst[:, b * 512:(b + 1) * 512]
        if dt == BF:
            v = v.bitcast(BF)
        return v

    def nb(self):
        b = self.rr % 4
        self.rr += 1
        return b

    def mm(self, out, lhsT, rhs, start=True, stop=True):
        self.S.add("pe", lambda e: e.matmul(out, lhsT, rhs, start=start, stop=stop), [lhsT, rhs], [out])

    def tr(self, out, in_, ident):
        self.S.add("pe", lambda e: e.transpose(out, in_, ident), [in_, ident], [out])

    def act(self, out, in_, func, bias=None, scale=None, accum=None):
        kw = {}
        rd = [in_]
        wr = [out]
        if bias is not None:
            kw["bias"] = bias
            if not isinstance(bias, float):
                rd.append(bias)
        if scale is not None:
            kw["scale"] = scale
            if not isinstance(scale, float):
                rd.append(scale)
        if accum is not None:
            kw["accum_out"] = accum
            wr.append(accum)
        self.S.add("act", lambda e: e.activation(out, in_, func, **kw), rd, wr)

    def tt(self, eng, out, in0, in1, op):
        self.S.add(eng, lambda e: e.tensor_tensor(out, in0, in1, op), [in0, in1], [out])

    def ts(self, eng, out, in0, s1, s2, op0, op1=None, accum=None):
        rd = [in0] + [s for s in (s1, s2) if s is not None and not isinstance(s, float)]
        wr = [out] + ([accum] if accum is not None else [])
        kw = {}
        if op1 is not None:
            kw["op1"] = op1
        if accum is not None:
            kw["accum_out"] = accum
        self.S.add(eng, lambda e: e.tensor_scalar(out, in0, s1, s2, op0, **kw), rd, wr)

    def stt(self, eng, out, in0, sc, in1, op0, op1):
        rd = [in0, in1] + ([] if isinstance(sc, float) else [sc])
        self.S.add(eng, lambda e: e.scalar_tensor_tensor(out, in0, sc, in1, op0, op1), rd, [out])

    def rstd(self, out, ss, n):
        eps = self.EPSC[0:out.shape[0], 0:1]
        self.S.add("act", lambda e: e.activation(out, ss, AF.Ln, bias=eps, scale=1.0 / n), [ss, eps], [out])
        self.S.add("act", lambda e: e.activation(out, out, AF.Exp, scale=-0.5), [out], [out])

    def cp(self, eng, out, in_):
        if eng == "act":
            self.S.add("act", lambda e: e.activation(out, in_, AF.Copy), [in_], [out])
        else:
            self.S.add(eng, lambda e: e.tensor_copy(out, in_), [in_], [out])

    def ms(self, eng, out, val):
        self.S.add(eng, lambda e: e.memset(out, val), [], [out])

    def dma(self, out, in_, rreg=None, wreg=None, eng="sp"):
        rd = [] if in_.tensor.name in ("x", "wsrc", "cst", "pf", "pr", "w2", "fnw") else [in_]
        if rreg is not None:
            rd = [rreg]
        wr = [out] if wreg is None else [wreg]
        self.S.add(eng, lambda e: e.dma_start(out=out, in_=in_), rd, wr, dma=True)

    @staticmethod
    def v3(ap, a, b):
        return ap.rearrange("p (a b) -> p a b", a=a, b=b)

    @staticmethod
    def bcast(ap, pairs):
        return bass.AP(ap.tensor, ap.offset, [list(ap.ap[0])] + [list(p) for p in pairs])

    def plan_weights(self):
        q = []
        for l in range(self.L):
            for t in range(self.NT):
                for name, kind, a in STREAM:
                    o, n = STREAM_OFF[name]
                    q.append((l, o, n, name))
        self.wq = q

    def _issue_w(self):
        i = self.wissued
        l, o, n, name = self.wq[i]
        buf = self.wbuf[i % NWB]
        self.dma(buf[:, 0:n], self.wbf[l, :, o:o + n], rreg=("wbf", l, l + 1, o * 2, (o + n) * 2))
        self.wissued += 1

    def next_w(self, expect):
        i = self.wnext
        l, o, n, name = self.wq[i]
        assert name == expect, (name, expect)
        while self.wissued < min(len(self.wq), i + NWB):
            self._issue_w()
        self.wnext += 1
        return self.wbuf[i % NWB][:, 0:n]

    def build(self):
        nc = self.nc
        TS, TT = self.TS, self.TT
        v3 = self.v3
        cst = self.alloc(3 * 128, F32)
        c3 = v3(cst, 3, 128)
        IDf, ULE, UGT = c3[:, 0, :], c3[:, 1, :], c3[:, 2, :]
        self.dma(cst, self.cst.rearrange("p a b -> p (a b)"))
        IDb = self.alloc(128, BF)
        self.cp("dve", IDb, IDf)
        ONESf = self.alloc(128, F32)
        self.ms("pool", ONESf, 1.0)
        ONESb = self.alloc(128, BF)
        self.ms("pool", ONESb, 1.0)
        self.IDb, self.ULE, self.UGT, self.ONESf, self.ONESb = IDb, ULE, UGT, ONESf, ONESb
        self.EPSC = self.alloc(16, F32)
        self.ms("pool", self.EPSC, EPS)
        PF = self.alloc(self.NPF, F32)
        self.n1T, self.n2T, self.gnT = PF[:, 0:8], PF[:, 8:16], PF[:, 16:18]
        self.cwT = v3(PF[:, 18:114], 24, 4)
        self.cbT, self.snT = PF[:, 114:138], PF[:, 138:154]
        self.PF = PF
        PRf = self.alloc(self.NPR, F32, parts=1)
        PRb = self.alloc(self.NPR, BF, parts=1)
        self.PRf, self.PRb = PRf, PRb
        ABC = self.alloc(64, F32)
        self.ABC = ABC
        self.AROW = self.alloc(32, F32)
        W2f = self.alloc(512, F32, parts=16)
        self.W2f = W2f
        self.W2b = self.alloc(512, BF, parts=16)
        self.DI = self.alloc(32 * 128, BF)
        self.FNW = self.alloc(D, F32)
        self.dma(self.FNW, self.fnw.partition_broadcast(128).rearrange("p a b -> p (a b)"))
        self.Sg = self.alloc(1024, F32)
        self.Sgb = self.alloc(1024, BF)
        self.SS = self.alloc(2048, F32)
        self.HIST = self.alloc(24 * 3, F32)
        self.X = self.alloc(TS * D, F32)
        self.hT = self.alloc(8 * TT, BF)
        self.JUNK = self.alloc(1024, BF)
        self.st = self.alloc(64, F32)
        self.wbuf = [self.alloc(WMAX, BF) for _ in range(NWB)]
        self.YAG = self.alloc(8 * TT, BF)
        base = self.top
        self.convert_layer(0, 0, NCHUNK)
        self.top = base
        self.alloc_phases()
        self.plan_weights()
        for l in range(self.L):
            self.layer_setup(l)
            for t in range(self.NT):
                if l + 1 < self.L:
                    per = (NCHUNK + self.NT - 1) // self.NT
                    self.convert_layer(l + 1, t * per, min(NCHUNK, (t + 1) * per))
                self.tile(l, t)
        self.S.emit()
        return nc

    def alloc_phases(self):
        TS, TT = self.TS, self.TT
        base = self.top
        self.GLRT = self.alloc(TT, BF, parts=16)
        self.alloc2("LG", 512, F32)
        self.alloc2("E1", 512, F32)
        self.EBT = self.alloc(4 * TT, F32)
        self.ENBT = self.alloc(4 * TT, F32)
        self.ED = self.alloc(TS * 512, F32)
        self.QGT = self.alloc(4 * TT, BF)
        self.KGT = self.alloc(4 * TT, BF)
        self.KD = self.alloc(TS * 512, BF)
        self.V = self.alloc(TS * 1024, BF)
        self.SRW = self.alloc(8 * TT, BF)
        self.OGT = self.alloc(8 * TT, BF)
        self.alloc2("ATM", 512, BF)
        self.alloc2("ON", 1024, BF)
        self.alloc2("SGT", TT, F32)
        self.alloc2("OF", 1024, F32)
        gla_top = self.top
        self.top = base
        self.DT = self.alloc(TS * 32, F32)
        self.AA = self.alloc(TS * 32, F32)
        self.ECUM = self.alloc(TS * 32, F32)
        self.DEC = self.alloc(TS * 32, F32)
        self.ECL = self.alloc(TS * 32, F32)
        self.BT = self.alloc(4 * TT, BF)
        self.CT = self.alloc(4 * TT, BF)
        self.BTOK = self.alloc(TS * 512, BF)
        self.CBM = self.alloc(TS * 512, BF)
        off_xr = self.top
        self.XR = self.alloc(4 * (TT + 4), F32)
        self.ACC = self.alloc(4 * TT, F32)
        self.XC = self.alloc(4 * TT, BF)
        self.XS = self.alloc(TS * 512, BF)
        self.XDT = self.alloc(TS * 512, BF)
        self.SZ = self.alloc(TS * 512, BF)
        self.alloc2("AE", 512, F32)
        self.alloc2("EX", 512, F32, n=4)
        self.MTA = self.alloc(TS * 1024, BF)
        self.SSBV = self.alloc(TS * 512, BF)
        self.alloc2("YO", 512, F32)
        self.alloc2("YN", 512, BF)
        self.alloc2("XDD", 512, BF)
        off_ynt = self.top
        self.YNT = self.alloc(16 * TT, BF)
        self.XSTG = self.arena[:, off_ynt // 2: off_ynt // 2 + TS * D * 2].bitcast(F32)
        self.SGB = self.arena[:, off_xr // 2: off_xr // 2 + 8 * TT]
        ssd_top = self.top
        self.top = base
        self.ACTT = self.alloc(22 * TT, BF)
        self.SG = [self.alloc(TT, F32) for _ in range(2)]
        ffn_top = self.top
        self.top = max(gla_top, ssd_top, ffn_top)
        self.phase_tops = (base, gla_top, ssd_top, ffn_top)

    def convert_layer(self, l, c_lo, c_hi):
        csz = (NWL + NCHUNK - 1) // NCHUNK
        for c in range(c_lo, c_hi):
            c0 = c * csz
            n = min(csz, NWL - c0)
            if n <= 0:
                continue
            self.dma(self.wbf[l, :, c0:c0 + n], self.wsrc[l, :, c0:c0 + n],
                     wreg=("wbf", l, l + 1, c0 * 2, (c0 + n) * 2), eng="pool")

    def layer_setup(self, l):
        v3 = self.v3
        self.dma(self.PF, self.pf[l])
        self.dma(self.PRf, self.pr[l])
        self.cp("pool", self.PRb, self.PRf)
        self.dma(self.W2f, self.w2[l])
        self.cp("pool", self.W2b, self.W2f)
        self.dma(self.ABC, self.pr[l, :, 544:608].partition_broadcast(128).rearrange("p a b -> p (a b)"))
        self.act(self.AROW, self.ABC[:, 0:32], AF.Exp)
        self.ts("dve", self.AROW, self.AROW, -1.0, None, ALU.mult)
        DI3 = v3(self.DI, 32, 128)
        idb = self.bcast(self.IDb, [[0, 32], [1, 128]])
        dsk = self.bcast(self.ABC[:, 32:64], [[1, 32], [0, 128]])
        self.tt("pool", DI3, idb, dsk, ALU.mult)
        self.ms("pool", self.Sg, 0.0)
        self.ms("pool", self.Sgb, 0.0)
        self.ms("pool", self.SS, 0.0)
        self.ms("pool", self.HIST, 0.0)

    def norm_to_hT(self, nT, src=None):
        TS, TT = self.TS, self.TT
        X3 = self.v3(self.X if src is None else src, TS, D)
        hT3 = self.v3(self.hT, 8, TT)
        ss = self.st[:, 0:TS]
        rs = self.st[:, 8:8 + TS]
        for s in range(TS):
            self.act(self.JUNK, X3[:, s, :], AF.Square, accum=ss[:, s:s + 1])
        self.rstd(rs, ss, D)
        for s in range(TS):
            self.rot("ON")
            self.act(self.ON, X3[:, s, :], AF.Copy, scale=rs[:, s:s + 1])
            b = 4 + (s % 2)
            tb = self.bank(b, BF)
            for c in range(8):
                self.tr(tb[:, c * 128:(c + 1) * 128], self.ON[:, c * 128:(c + 1) * 128], self.IDb)
            self.tt("dve", hT3[:, :, s * 128:(s + 1) * 128], self.v3(tb, 8, 128),
                    self.bcast(nT, [[1, 8], [0, 128]]), ALU.mult)

    def proj_fm(self, W3, c0, rhs3, nk, bank):
        out = self.bank(bank)[:, 0:self.TT]
        for k in range(nk):
            self.mm(out, W3[:, k, c0:c0 + 128], rhs3[:, k, :], start=(k == 0), stop=(k == nk - 1))
        return out

    def proj_tm(self, lhs3, s, W3, ncols, nk, bank):
        out = self.bank(bank)[:, 0:ncols]
        for k in range(nk):
            self.mm(out, lhs3[:, k, s * 128:(s + 1) * 128], W3[:, k, 0:ncols], start=(k == 0), stop=(k == nk - 1))
        return out

    def tile(self, l, t):
        TS, TT = self.TS, self.TT
        v3 = self.v3
        bc = self.bcast
        X3 = v3(self.X, TS, D)
        hT3 = v3(self.hT, 8, TT)
        XSTG3 = v3(self.XSTG, TS, D)
        if l == 0 and t == 0:
            self.dma(XSTG3, self.x_in[0:TT, :].rearrange("(s p) d -> p s d", p=128))
        self.dma(self.X, self.XSTG)
        import os
        stage = int(os.environ.get("KSTAGE", "99"))
        if stage <= 0:
            self.dma(self.y_out[t * TT:(t + 1) * TT, :].rearrange("(s p) d -> p s d", p=128), X3)
            return
        self.norm_to_hT(self.n1T, src=self.XSTG)
        def stop(n):
            if stage <= n:
                self.dma(self.y_out[t * TT:(t + 1) * TT, :].rearrange("(s p) d -> p s d", p=128), X3)
                return True
            return False
        if stop(1):
            return

        W = v3(self.next_w("SM"), 8, 48)
        b = self.nb()
        o = self.bank(b)[0:16, 0:TT]
        for k in range(8):
            self.mm(o, W[:, k, 0:16], hT3[:, k, :], start=(k == 0), stop=(k == 7))
        self.cp("act", self.GLRT, o)
        DT3 = v3(self.DTp, TS, 32)
        b = self.nb()
        dtb = self.bank(b)
        for s in range(TS):
            o = dtb[:, s * 32:(s + 1) * 32]
            for k in range(8):
                self.mm(o, hT3[:, k, s * 128:(s + 1) * 128], W[:, k, 16:48], start=(k == 0), stop=False)
            self.mm(o, self.ONESb[0:1, 0:128], self.PRb[0:1, 512:544], start=False, stop=True)
        self.act(self.DTe, dtb[:, 0:TS * 32], AF.Exp)
        self.act(self.DTp, self.DTe, AF.Ln, bias=1.0)

        if stop(2):
            return
        EBT3, ENBT3, ED3 = v3(self.EBT, 4, TT), v3(self.ENBT, 4, TT), v3(self.ED, TS, 512)
        for s in range(TS):
            sc = slice(s * 128, (s + 1) * 128)
            self.rot("E1")
            self.rot("LG")
            b = self.nb()
            lg = self.bank(b)
            self.mm(lg, self.GLRT[0:16, sc], self.W2b[0:16, :], start=True, stop=False)
            self.mm(lg, self.ONESb[0:1, 0:128], self.PRb[0:1, 0:512], start=False, stop=True)
            self.act(self.E1, lg, AF.Exp, scale=-1.0)
            self.act(self.LG, self.E1, AF.Ln, bias=1.0)
            b = self.nb()
            bt = self.bank(b)
            for h in range(4):
                self.mm(bt[:, h * 128:(h + 1) * 128], self.LG[:, h * 128:(h + 1) * 128], self.ULE)
            self.act(EBT3[:, :, sc], v3(bt, 4, 128), AF.Exp, scale=-1.0 / 16)
            self.act(ENBT3[:, :, sc], v3(bt, 4, 128), AF.Exp, scale=1.0 / 16)
            b = self.nb()
            dr = self.bank(b)
            self.mm(dr, self.UGT, self.LG)
            self.act(ED3[:, s, :], dr, AF.Exp, scale=-1.0 / 16)

        if stop(3):
            return
        QGT3, KGT3, KD3, V3_ = v3(self.QGT, 4, TT), v3(self.KGT, 4, TT), v3(self.KD, TS, 512), v3(self.V, TS, 1024)
        W = v3(self.next_w("Q"), 8, 512)
        for h in range(4):
            o = self.proj_fm(W, h * 128, hT3, 8, self.nb())
            self.stt("dve", QGT3[:, h, :], o, 128.0 ** -0.5, EBT3[:, h, :], ALU.mult, ALU.mult)
        W = v3(self.next_w("K"), 8, 512)
        for h in range(4):
            o = self.proj_fm(W, h * 128, hT3, 8, self.nb())
            self.tt("dve", KGT3[:, h, :], o, ENBT3[:, h, :], ALU.mult)
        for s in range(TS):
            o = self.proj_tm(hT3, s, W, 512, 8, self.nb())
            self.tt("dve", KD3[:, s, :], o, ED3[:, s, :], ALU.mult)
        for cb in range(2):
            W = v3(self.next_w("V%d" % cb), 8, 512)
            for s in range(TS):
                o = self.proj_tm(hT3, s, W, 512, 8, self.nb())
                self.cp("act", V3_[:, s, cb * 512:(cb + 1) * 512], o)
        SRW3 = v3(self.SRW, 8, TT)
        for cb in range(2):
            W = v3(self.next_w("R%d" % cb), 8, 512)
            for fb in range(4):
                j = cb * 4 + fb
                o = self.proj_fm(W, fb * 128, hT3, 8, self.nb())
                self.rot("SGT")
                self.act(self.SGT, o, AF.Silu)
                self.ts("dve", SRW3[:, j, :], self.SGT, self.gnT[:, j % 2:j % 2 + 1], None, ALU.mult)

        if stop(4):
            return
        S3, Sb3 = v3(self.Sg, 4, 256), v3(self.Sgb, 4, 256)
        OGT3 = v3(self.OGT, 8, TT)
        for s in range(TS):
            sc = slice(s * 128, (s + 1) * 128)
            ATM3 = v3(self.rot("ATM"), 4, 128)
            self.rot("OF")
            self.rot("ON")
            oss = self.st[:, 16 + 8 * (s % 2):20 + 8 * (s % 2)]
            ors = self.st[:, 20 + 8 * (s % 2):24 + 8 * (s % 2)]
            b = self.nb()
            at = self.bank(b)
            for h in range(4):
                self.mm(at[:, h * 128:(h + 1) * 128], KGT3[:, h, sc], QGT3[:, h, sc])
            self.tt("dve", ATM3, v3(at, 4, 128), bc(self.ULE, [[0, 4], [1, 128]]), ALU.mult)
            sub = int(os.environ.get("KSUB", "99"))
            if sub <= 0:
                continue
            ob = [self.bank(4), self.bank(5)]
            db = [self.bank(6), self.bank(7)]
            for h in range(4):
                o = ob[h // 2][:, (h % 2) * 256:(h % 2 + 1) * 256]
                self.mm(o, ATM3[:, h, :], V3_[:, s, h * 256:(h + 1) * 256], start=True, stop=False)
                self.mm(o, QGT3[:, h, sc], Sb3[:, h, :], start=False, stop=True)
            if sub <= 1:
                continue
            for h in range(4):
                d = db[h // 2][:, (h % 2) * 256:(h % 2 + 1) * 256]
                self.mm(d, KD3[:, s, h * 128:(h + 1) * 128], V3_[:, s, h * 256:(h + 1) * 256])
            if sub <= 2:
                continue
            for h in range(4):
                d = db[h // 2][:, (h % 2) * 256:(h % 2 + 1) * 256]
                self.stt("dve", S3[:, h, :], S3[:, h, :], EBT3[:, h, s * 128 + 127:s * 128 + 128], d, ALU.mult, ALU.add)
                self.cp("act", Sb3[:, h, :], S3[:, h, :])
            if sub <= 3:
                continue
            for h in range(4):
                o = ob[h // 2][:, (h % 2) * 256:(h % 2 + 1) * 256]
                self.cp("dve", self.OF[:, h * 256:(h + 1) * 256], o)
                self.act(self.JUNK[:, 0:256], self.OF[:, h * 256:(h + 1) * 256], AF.Square, accum=oss[:, h:h + 1])
            sub2 = int(os.environ.get("KSUB2", "99"))
            if sub2 <= 0:
                continue
            self.rstd(ors, oss, 256)
            if sub2 <= 1:
                continue
            for h in range(4):
                o = ob[h // 2][:, (h % 2) * 256:(h % 2 + 1) * 256]
                self.act(self.ON[:, h * 256:(h + 1) * 256], self.OF[:, h * 256:(h + 1) * 256], AF.Copy, scale=ors[:, h:h + 1])
            if sub <= 4:
                continue
            b = self.nb()
            tb = self.bank(b, BF)
            for j in range(8):
                self.tr(tb[:, j * 128:(j + 1) * 128], self.ON[:, j * 128:(j + 1) * 128], self.IDb)
            self.tt("dve", OGT3[:, :, sc], v3(tb, 8, 128), SRW3[:, :, sc], ALU.mult)

        if stop(5):
            return
        YAG3 = v3(self.YAG, 8, TT)
        for cb in range(2):
            W = v3(self.next_w("GA%d" % cb), 8, 512)
            for fb in range(4):
                o = self.proj_fm(W, fb * 128, hT3, 8, self.nb())
                self.act(YAG3[:, cb * 4 + fb, :], o, AF.Sigmoid)
        for cb in range(2):
            W = v3(self.next_w("WYA%d" % cb), 8, 512)
            for fb in range(4):
                o = self.proj_fm(W, fb * 128, OGT3, 8, self.nb())
                self.tt("dve", YAG3[:, cb * 4 + fb, :], o, YAG3[:, cb * 4 + fb, :], ALU.mult)

        if stop(6):
            return
        self.ssd(l, t)
        if stop(7):
            return

        SGB3 = v3(self.SGB, 8, TT)
        YNT3 = v3(self.YNT, 16, TT)
        for cb in range(2):
            W = v3(self.next_w("GB%d" % cb), 8, 512)
            for fb in range(4):
                o = self.proj_fm(W, fb * 128, hT3, 8, self.nb())
                self.act(SGB3[:, cb * 4 + fb, :], o, AF.Sigmoid)
        for cb in range(4):
            W = v3(self.next_w("WYB%d" % cb), 16, 256)
            for fb in range(2):
                j = cb * 2 + fb
                o = self.proj_fm(W, fb * 128, YNT3, 16, self.nb())
                self.tt("dve", SGB3[:, j, :], o, SGB3[:, j, :], ALU.mult)
                self.tt("dve", SGB3[:, j, :], SGB3[:, j, :], YAG3[:, j, :], ALU.add)
        for cb in range(2):
            W = v3(self.next_w("WO%d" % cb), 8, 512)
            for s in range(TS):
                o = self.proj_tm(SGB3, s, W, 512, 8, self.nb())
                xs_ = X3[:, s, cb * 512:(cb + 1) * 512]
                self.tt("dve", xs_, xs_, o, ALU.add)

        if stop(8):
            return
        self.norm_to_hT(self.n2T)
        nl, nt = (l, t + 1) if t + 1 < self.NT else (l + 1, 0)
        if nl < self.L:
            nsrc = self.x_in if nl == 0 else self.xscr
            self.dma(XSTG3, nsrc[nt * TT:(nt + 1) * TT, :].rearrange("(s p) d -> p s d", p=128))
        ACTT3 = v3(self.ACTT, 22, TT)
        for i in range(11):
            W = v3(self.next_w("FI%d" % i), 8, 512)
            for q in range(2):
                og = self.proj_fm(W, q * 128, hT3, 8, self.nb())
                ou = self.proj_fm(W, 256 + q * 128, hT3, 8, self.nb())
                sg = self.SG[q]
                self.act(sg, og, AF.Silu)
                self.tt("dve", ACTT3[:, i * 2 + q, :], sg, ou, ALU.mult)
        for cb in range(4):
            W = v3(self.next_w("FO%d" % cb), 22, 256)
            for s in range(TS):
                o = self.proj_tm(ACTT3, s, W, 256, 22, self.nb())
                xs_ = X3[:, s, cb * 256:(cb + 1) * 256]
                self.tt("dve", xs_, xs_, o, ALU.add)

        rows = slice(t * TT, (t + 1) * TT)
        if l == self.L - 1 and self.final:
            ss = self.st[:, 0:TS]
            rs = self.st[:, 8:8 + TS]
            for s in range(TS):
                self.act(self.JUNK, X3[:, s, :], AF.Square, accum=ss[:, s:s + 1])
            self.rstd(rs, ss, D)
            for s in range(TS):
                self.stt("dve", X3[:, s, :], X3[:, s, :], rs[:, s:s + 1], self.FNW, ALU.mult, ALU.mult)
            self.dma(self.y_out[rows, :].rearrange("(s p) d -> p s d", p=128), X3)
        elif l == self.L - 1:
            self.dma(self.y_out[rows, :].rearrange("(s p) d -> p s d", p=128), X3)
        else:
            self.dma(self.xscr[rows, :].rearrange("(s p) d -> p s d", p=128), X3)

    def ssd(self, l, t):
        TS, TT = self.TS, self.TT
        v3 = self.v3
        bc = self.bcast
        hT3 = v3(self.hT, 8, TT)
        DT3, AA3 = v3(self.DT, TS, 32), v3(self.AA, TS, 32)
        self.cp("pool", self.DT, self.DTp)
        self.tt("dve", AA3, DT3, bc(self.AROW, [[0, TS], [1, 32]]), ALU.mult)
        b = self.nb()
        cb_ = self.bank(b)
        for s in range(TS):
            a_s = self.AA[:, s * 32:(s + 1) * 32]
            self.mm(cb_[:, s * 32:(s + 1) * 32], self.ULE, a_s)
            self.mm(cb_[:, 128 + s * 32:128 + (s + 1) * 32], self.UGT, a_s)
            self.mm(cb_[:, 256 + s * 32:256 + (s + 1) * 32], self.ONESf, a_s)
        self.act(self.ECUM, cb_[:, 0:TS * 32], AF.Exp)
        self.act(self.DEC, cb_[:, 128:128 + TS * 32], AF.Exp)
        self.act(self.ECL, cb_[:, 256:256 + TS * 32], AF.Exp)
        ECUM3, DEC3, ECL3 = v3(self.ECUM, TS, 32), v3(self.DEC, TS, 32), v3(self.ECL, TS, 32)

        import os
        kssd = int(os.environ.get("KSSD", "99"))
        if kssd <= 0:
            return
        XR3 = v3(self.XR, 4, TT + 4)
        ACC3 = v3(self.ACC, 4, TT)
        HIST3 = v3(self.HIST, 24, 3)

        def conv_proj(W, fb0):
            self.cp("pool", XR3[:, :, 0:3], HIST3[:, fb0:fb0 + 4, :])
            for fb in range(4):
                o = self.proj_fm(W, fb * 128, hT3, 8, self.nb())
                self.cp("act", XR3[:, fb, 3:3 + TT], o)
            self.cp("pool", HIST3[:, fb0:fb0 + 4, :], XR3[:, :, TT:TT + 3])

        def conv_apply(fb0, out3):
            for fb in range(4):
                cw = self.cwT[:, fb0 + fb, :]
                self.ts("dve", ACC3[:, fb, :], XR3[:, fb, 0:TT], cw[:, 0:1], None, ALU.mult)
                for k in range(1, 4):
                    self.stt("dve", ACC3[:, fb, :], XR3[:, fb, k:k + TT], cw[:, k:k + 1], ACC3[:, fb, :], ALU.mult, ALU.add)
                self.act(out3[:, fb, :], ACC3[:, fb, :], AF.Silu, bias=self.cbT[:, fb0 + fb:fb0 + fb + 1])

        def conv_block(W, fb0, out3):
            conv_proj(W, fb0)
            conv_apply(fb0, out3)

        BT3, CT3 = v3(self.BT, 4, TT), v3(self.CT, 4, TT)
        conv_block(v3(self.next_w("B"), 8, 512), 16, BT3)
        conv_block(v3(self.next_w("C"), 8, 512), 20, CT3)
        if kssd <= 1:
            return
        BTOK3 = v3(self.BTOK, TS, 512)
        CBM = self.CBM
        for s in range(TS):
            sc = slice(s * 128, (s + 1) * 128)
            b = self.nb()
            tb = self.bank(b, BF)
            for g in range(4):
                self.tr(tb[:, g * 128:(g + 1) * 128], BT3[:, g, sc], self.IDb)
            self.cp("act", BTOK3[:, s, :], tb[:, 0:512])
            b = self.nb()
            cbk = self.bank(b)
            for g in range(4):
                self.mm(cbk[:, g * 128:(g + 1) * 128], BT3[:, g, sc], CT3[:, g, sc])
            self.tt("dve", v3(CBM[:, s * 512:(s + 1) * 512], 4, 128), v3(cbk, 4, 128),
                    bc(self.ULE, [[0, 4], [1, 128]]), ALU.mult)

        if kssd <= 2:
            return
        XC3 = v3(self.XC, 4, TT)
        XS3, XDT3, SZ3 = v3(self.XS, TS, 512), v3(self.XDT, TS, 512), v3(self.SZ, TS, 512)
        YNT3 = v3(self.YNT, 16, TT)
        DI3 = v3(self.DI, 32, 128)
        SS3 = v3(self.SS, 4, 512)
        yss = self.st[:, 24:25]
        yrs = self.st[:, 25:26]
        for g in range(4):
            conv_proj(v3(self.next_w("XS%d" % g), 8, 512), g * 4)
            W = v3(self.next_w("Z%d" % g), 8, 512)
            for s in range(TS):
                o = self.proj_tm(hT3, s, W, 512, 8, self.nb())
                self.act(SZ3[:, s, :], o, AF.Silu)
            conv_apply(g * 4, XC3)
            for s in range(TS):
                sc = slice(s * 128, (s + 1) * 128)
                tb = self.bank(4 + (s % 2), BF)
                for fb in range(4):
                    self.tr(tb[:, fb * 128:(fb + 1) * 128], XC3[:, fb, sc], self.IDb)
                self.cp("act", XS3[:, s, :], tb[:, 0:512])
                if os.environ.get("KNOXDT") != "1":
                    self.tt("dve", v3(XDT3[:, s, :], 8, 64), v3(XS3[:, s, :], 8, 64),
                            bc(DT3[:, s, g * 8:(g + 1) * 8], [[1, 8], [0, 64]]), ALU.mult)
            SSBV3 = v3(self.SSBV, TS, 512)
            MTA5 = self.MTA.rearrange("p (s h e i) -> p s h e i", s=TS, h=2, e=4, i=128)
            self.cp("act", SSBV3[:, 0, :], SS3[:, g, :])
            pend = []

            def flush_mt():
                for (ss_, half_, ex_) in pend:
                    cbm = CBM[:, ss_ * 512 + g * 128: ss_ * 512 + (g + 1) * 128]
                    self.tt("dve", MTA5[:, ss_, half_, :, :], v3(ex_, 4, 128), bc(cbm, [[0, 4], [1, 128]]), ALU.mult)
                del pend[:]

            for s in range(TS):
                self.rot("XDD")
                self.tt("dve", v3(self.XDD, 8, 64), v3(XDT3[:, s, :], 8, 64),
                        bc(DEC3[:, s, g * 8:(g + 1) * 8], [[1, 8], [0, 64]]), ALU.mult)
                ds = self.bank(self.nb())
                self.mm(ds, BTOK3[:, s, g * 128:(g + 1) * 128], self.XDD)
                newp = []
                for half in range(2):
                    e0 = g * 8 + half * 4
                    self.rot("AE")
                    self.rot("EX")
                    AE3 = v3(self.AE, 4, 128)
                    self.tt("dve", AE3, bc(self.UGT, [[0, 4], [1, 128]]),
                            bc(AA3[:, s, e0:e0 + 4], [[1, 4], [0, 128]]), ALU.mult)
                    sb = self.bank(self.nb())
                    for ei in range(4):
                        self.mm(sb[:, ei * 128:(ei + 1) * 128], AE3[:, ei, :], self.ULE)
                    self.act(self.EX, sb, AF.Exp)
                    newp.append((s, half, self.EX))
                flush_mt()
                pend.extend(newp)
                self.tt("dve", v3(SS3[:, g, :], 8, 64), v3(SS3[:, g, :], 8, 64),
                        bc(ECL3[:, s, g * 8:(g + 1) * 8], [[1, 8], [0, 64]]), ALU.mult)
                self.tt("dve", SS3[:, g, :], SS3[:, g, :], ds, ALU.add)
                if s < TS - 1:
                    self.cp("act", SSBV3[:, s + 1, :], SS3[:, g, :])
            flush_mt()

            st_b2 = []
            st_b3 = []

            def run_b3():
                for (ss_, yn_) in st_b3:
                    tb = self.bank(4 + (ss_ % 2), BF)
                    for fb in range(4):
                        self.tr(tb[:, fb * 128:(fb + 1) * 128], yn_[:, fb * 128:(fb + 1) * 128], self.IDb)
                    self.tt("dve", YNT3[:, g * 4:(g + 1) * 4, ss_ * 128:(ss_ + 1) * 128], v3(tb[:, 0:512], 4, 128),
                            bc(self.snT[:, g * 4:(g + 1) * 4], [[1, 4], [0, 128]]), ALU.mult)
                del st_b3[:]

            def run_b2():
                for (ss_, yo_, yn_, yss_, yrs_) in st_b2:
                    self.rstd(yrs_, yss_, 512)
                    self.act(yn_, yo_, AF.Copy, scale=yrs_)
                    st_b3.append((ss_, yn_))
                del st_b2[:]

            for s in range(TS):
                sc = slice(s * 128, (s + 1) * 128)
                self.rot("YO")
                self.rot("YN")
                k3 = s % 3
                yss = self.st[:, 40 + k3 * 2:41 + k3 * 2]
                yrs = self.st[:, 41 + k3 * 2:42 + k3 * 2]
                yo = self.bank(self.nb())
                self.mm(yo, CT3[:, g, sc], SSBV3[:, s, :])
                self.cp("act", self.YO, yo)
                self.tt("dve", v3(self.YO, 8, 64), v3(self.YO, 8, 64),
                        bc(ECUM3[:, s, g * 8:(g + 1) * 8], [[1, 8], [0, 64]]), ALU.mult)
                yb = self.bank(6 + (s % 2))
                for half in range(2):
                    e0 = g * 8 + half * 4
                    for ei in range(4):
                        c0 = (half * 4 + ei) * 64
                        self.mm(yb[:, c0:c0 + 64], MTA5[:, s, half, ei, :], XDT3[:, s, c0:c0 + 64], start=True, stop=False)
                        self.mm(yb[:, c0:c0 + 64], DI3[:, e0 + ei, :], XS3[:, s, c0:c0 + 64], start=False, stop=True)
                run_b3()
                self.tt("dve", self.YO, yb, self.YO, ALU.add)
                self.tt("dve", self.YO, self.YO, SZ3[:, s, :], ALU.mult)
                self.act(self.JUNK[:, 0:512], self.YO, AF.Square, accum=yss)
                run_b2()
                st_b2.append((s, self.YO, self.YN, yss, yrs))
            run_b3()
            run_b2()
            run_b3()


def build_program(n_layers=DEPTH, n_tiles=8, TS=4, final=True):
    B = Builder(n_layers, n_tiles, TS, final)
    B.DTp = B.alloc(TS * 32, F32)
    B.DTe = B.alloc(TS * 32, F32)
    B.build()
    return B.nc


def host_consts():
    c = np.zeros((128, 3, 128), np.float32)
    j = np.arange(128)[:, None]
    i = np.arange(128)[None, :]
    c[:, 0, :] = (j == i)
    c[:, 1, :] = (j <= i)
    c[:, 2, :] = (j > i)
    return c


def host_params(inp, n_layers=DEPTH):
    fm = lambda v, nb: np.ascontiguousarray(v.reshape(nb, 128).T)
    pf, pr, ws = [], [], []
    for l in range(n_layers):
        cw = inp["ssm_conv_w"][l]
        cwT = np.ascontiguousarray(cw.reshape(4, 24, 128).transpose(2, 1, 0)).reshape(128, 96)
        pf.append(np.concatenate([fm(inp["norm1_w"][l], 8), fm(inp["norm2_w"][l], 8), fm(inp["gla_norm_w"][l], 2),
                                  cwT, fm(inp["ssm_conv_b"][l], 24), fm(inp["ssm_norm_w"][l], 16)], axis=1))
        pr.append(np.concatenate([inp["gla_gate_b"][l], inp["ssm_dt_bias"][l], inp["ssm_A_log"][l],
                                  inp["ssm_D"][l]])[None, :])
        ws.append(host_weight_stream(inp["w_in"][l], inp["w_branch_a"][l], inp["w_branch_b"][l],
                                     inp["w_mix_out"][l], inp["w_ffn_in"][l], inp["w_ffn_out"][l]))
    return (np.ascontiguousarray(np.stack(pf)).astype(np.float32), np.ascontiguousarray(np.stack(pr)).astype(np.float32),
            np.stack(ws))


_CACHE = {}


def kernel(**inputs):
    inp = {k: np.asarray(v) for k, v in inputs.items()}
    x = inp["x"]
    pf, pr, ws = host_params(inp)
    if "nc" not in _CACHE:
        _CACHE["nc"] = build_program()
    nc = _CACHE["nc"]
    shared = dict(wsrc=ws, cst=host_consts(), pf=pf, pr=pr, w2=np.ascontiguousarray(inp["gla_gate_w2"]),
                  fnw=np.ascontiguousarray(inp["final_norm_w"][None, :]))
    in_maps = [dict(shared, x=np.ascontiguousarray(x[c])) for c in range(NCORES)]
    res = run_bass_kernel_spmd(nc, in_maps, core_ids=list(range(NCORES)))
    return np.stack([np.asarray(r["y"]) for r in res.results]).astype(np.float32)
```
